# Optimizing a Trainium2 kernel written in Bass

```python
import math
import jax
import jax.numpy as jnp
from jax import lax
import numpy as np

D_MODEL = 2048
BATCH = 4
SEQ = 4096
DEPTH = 4

GRID_W = 64
CTX_LEN = 256
Q_BLOCK = 128
ROPE_THETA = 10000.0
EPS = 1e-6

HEAD_DIM = 128
BRANCH_WIDTH = 1024
N_BRANCHES = 3
GQA_HEADS = 8
GQA_KV_HEADS = 2
GQA_GROUP = GQA_HEADS // GQA_KV_HEADS
MLA_HEADS = 8
MLA_NOPE = 128
MLA_ROPE = 64
MLA_V = 128
MLA_KV_RANK = 512
DIFF_HEADS = 8
DIFF_QK = 64
DIFF_V = 128

IN_SIZES = (
    GQA_HEADS * HEAD_DIM,
    GQA_KV_HEADS * HEAD_DIM,
    GQA_KV_HEADS * HEAD_DIM,
    MLA_HEADS * (MLA_NOPE + MLA_ROPE),
    MLA_KV_RANK,
    MLA_ROPE,
    DIFF_HEADS * 2 * DIFF_QK,
    DIFF_HEADS * 2 * DIFF_QK,
    DIFF_HEADS * DIFF_V,
    N_BRANCHES * BRANCH_WIDTH,
    N_BRANCHES * D_MODEL,
)
IN_COLS = sum(IN_SIZES)

kernel_name = 'hybrid_gqa_mla_diffattn_prefix_block'


def rms_norm(x, w):
    x32 = x.astype(jnp.float32)
    y = x32 * lax.rsqrt(jnp.mean(jnp.square(x32), axis=-1, keepdims=True) + EPS)
    return (y * w.astype(jnp.float32)).astype(x.dtype)


def axial_rope_tables(pos_row, pos_col, rot_dim):
    axis_dim = rot_dim // 2
    inv_freq = ROPE_THETA ** (-jnp.arange(0, axis_dim, 2, dtype=jnp.float32) / axis_dim)
    ang_r = pos_row[:, None] * inv_freq
    ang_c = pos_col[:, None] * inv_freq
    ang = jnp.concatenate([ang_r, ang_r, ang_c, ang_c], axis=-1)
    return jnp.cos(ang), jnp.sin(ang)


def rotate_half(x):
    x1, x2 = jnp.split(x, 2, axis=-1)
    return jnp.concatenate([-x2, x1], axis=-1)


def apply_axial_rope(x, tab):
    cos, sin = tab
    shape = (cos.shape[0],) + (1,) * (x.ndim - 3) + (cos.shape[1],)
    cos = cos.reshape(shape).astype(x.dtype)
    sin = sin.reshape(shape).astype(x.dtype)
    xr, xc = jnp.split(x, 2, axis=-1)
    rot = jnp.concatenate([rotate_half(xr), rotate_half(xc)], axis=-1)
    return x * cos + rot * sin


def attn_probs(q, k):
    s = jnp.einsum('bqhgd,bkhd->bhgqk', q.astype(jnp.float32), k.astype(jnp.float32))
    return jax.nn.softmax(s * (q.shape[-1] ** -0.5), axis=-1)


def attn_apply(p, v):
    return jnp.einsum('bhgqk,bkhd->bqhgd', p.astype(v.dtype), v)


def sweep_query_blocks(fn, qs):
    b, n = qs[0].shape[:2]
    nblk = n // Q_BLOCK
    to_blocks = lambda a: jnp.moveaxis(a.reshape((b, nblk, Q_BLOCK) + a.shape[2:]), 1, 0)
    from_blocks = lambda a: jnp.moveaxis(a, 0, 1).reshape((b, n) + a.shape[3:])
    out = lax.map(fn, tuple(to_blocks(q) for q in qs))
    return jax.tree_util.tree_map(from_blocks, out)


def stream_qkv(h, p, tabs):
    b, n, _ = h.shape
    splits = [int(i) for i in np.cumsum(IN_SIZES)[:-1]]
    (gq, gk, gv, mq, mckv, mkr, dq, dk, dv, gate_in, merge_in) = jnp.split(h @ p['w_in'], splits, axis=-1)
    gq = rms_norm(gq.reshape(b, n, GQA_KV_HEADS, GQA_GROUP, HEAD_DIM), p['gqa_q_norm'])
    gk = rms_norm(gk.reshape(b, n, GQA_KV_HEADS, HEAD_DIM), p['gqa_k_norm'])
    gv = gv.reshape(b, n, GQA_KV_HEADS, HEAD_DIM)
    mq = mq.reshape(b, n, MLA_HEADS, 1, MLA_NOPE + MLA_ROPE)
    mq_nope = rms_norm(mq[..., :MLA_NOPE], p['mla_q_nope_norm'])
    mq_rope = rms_norm(mq[..., MLA_NOPE:], p['mla_q_rope_norm'])
    c_kv = rms_norm(mckv, p['mla_kv_norm'])
    mk_nope = rms_norm((c_kv @ p['mla_w_uk']).reshape(b, n, MLA_HEADS, MLA_NOPE), p['mla_k_nope_norm'])
    mv = (c_kv @ p['mla_w_uv']).reshape(b, n, MLA_HEADS, MLA_V)
    mk_rope = rms_norm(mkr, p['mla_k_rope_norm'])[:, :, None, :]
    dq = rms_norm(dq.reshape(b, n, DIFF_HEADS, 2, DIFF_QK), p['diff_q_norm'])
    dk = rms_norm(dk.reshape(b, n, DIFF_HEADS, 2, DIFF_QK), p['diff_k_norm'])
    dv = dv.reshape(b, n, DIFF_HEADS, DIFF_V)
    if tabs is not None:
        gq = apply_axial_rope(gq, tabs[HEAD_DIM])
        gk = apply_axial_rope(gk, tabs[HEAD_DIM])
        mq_rope = apply_axial_rope(mq_rope, tabs[MLA_ROPE])
        mk_rope = apply_axial_rope(mk_rope, tabs[MLA_ROPE])
        dq = apply_axial_rope(dq, tabs[DIFF_QK])
        dk = apply_axial_rope(dk, tabs[DIFF_QK])
    mq = jnp.concatenate([mq_nope, mq_rope], axis=-1)
    mk = jnp.concatenate([mk_nope, jnp.broadcast_to(mk_rope, (b, n, MLA_HEADS, MLA_ROPE))], axis=-1)
    queries = (gq, mq, dq[:, :, :, 0:1, :], dq[:, :, :, 1:2, :])
    kv = (gk, gv, mk, mv, dk[:, :, :, 0, :], dk[:, :, :, 1, :], dv)
    return queries, kv, gate_in, merge_in


def attend_all(queries, kv, lam):
    gq, mq, dq1, dq2 = queries
    gk, gv, mk, mv, dk1, dk2, dv = kv
    o_a = attn_apply(attn_probs(gq, gk), gv)
    o_b = attn_apply(attn_probs(mq, mk), mv)
    o_c = attn_apply(attn_probs(dq1, dk1) - lam * attn_probs(dq2, dk2), dv)
    return o_a, o_b, o_c


def merge_branches(outs, gate_in, merge_in, p, lam_init):
    o_a, o_b, o_c = outs
    b, n = gate_in.shape[:2]
    o_c = rms_norm(o_c.reshape(b, n, DIFF_HEADS, DIFF_V), p['diff_subln']) * (1.0 - lam_init)
    branches = (o_a.reshape(b, n, BRANCH_WIDTH), o_b.reshape(b, n, BRANCH_WIDTH), o_c.reshape(b, n, BRANCH_WIDTH))
    g = jnp.split(jax.nn.silu(gate_in), N_BRANCHES, axis=-1)
    m = jnp.split(jax.nn.sigmoid(merge_in + p['b_merge']), N_BRANCHES, axis=-1)
    y = (m[0] * ((branches[0] * g[0]) @ p['w_br_gqa'])
         + m[1] * ((branches[1] * g[1]) @ p['w_br_mla'])
         + m[2] * ((branches[2] * g[2]) @ p['w_br_diff']))
    return y @ p['w_out']


def setup_inputs(seed: int = 0) -> dict:
    key = jax.random.key(seed)
    ks = jax.random.split(key, 32)
    counter = iter(range(32))

    def nrm(shape, scale):
        return jax.random.normal(ks[next(counter)], shape, jnp.float32) * scale

    def gain(dim):
        return 1.0 + nrm((DEPTH, dim), 0.02)

    L = DEPTH
    return {
        'x': nrm((BATCH, SEQ, D_MODEL), 1.0),
        'c': nrm((BATCH, D_MODEL), 1.0),
        'ctx': nrm((BATCH, CTX_LEN, D_MODEL), 1.0),
        'c_ctx': nrm((D_MODEL,), 1.0),
        'norm_w': gain(D_MODEL),
        'w_ada': nrm((L, D_MODEL, 3 * D_MODEL), 0.5 * D_MODEL ** -0.5),
        'b_ada': nrm((L, 3 * D_MODEL), 0.01),
        'w_in': nrm((L, D_MODEL, IN_COLS), D_MODEL ** -0.5),
        'b_merge': nrm((L, N_BRANCHES * D_MODEL), 0.01),
        'gqa_q_norm': gain(HEAD_DIM),
        'gqa_k_norm': gain(HEAD_DIM),
        'mla_q_nope_norm': gain(MLA_NOPE),
        'mla_q_rope_norm': gain(MLA_ROPE),
        'mla_kv_norm': gain(MLA_KV_RANK),
        'mla_w_uk': nrm((L, MLA_KV_RANK, MLA_HEADS * MLA_NOPE), MLA_KV_RANK ** -0.5),
        'mla_w_uv': nrm((L, MLA_KV_RANK, MLA_HEADS * MLA_V), MLA_KV_RANK ** -0.5),
        'mla_k_nope_norm': gain(MLA_NOPE),
        'mla_k_rope_norm': gain(MLA_ROPE),
        'diff_q_norm': gain(DIFF_QK),
        'diff_k_norm': gain(DIFF_QK),
        'diff_lambda_q1': nrm((L, DIFF_QK), 0.1),
        'diff_lambda_k1': nrm((L, DIFF_QK), 0.1),
        'diff_lambda_q2': nrm((L, DIFF_QK), 0.1),
        'diff_lambda_k2': nrm((L, DIFF_QK), 0.1),
        'diff_subln': gain(DIFF_V),
        'w_br_gqa': nrm((L, BRANCH_WIDTH, D_MODEL), BRANCH_WIDTH ** -0.5),
        'w_br_mla': nrm((L, BRANCH_WIDTH, D_MODEL), BRANCH_WIDTH ** -0.5),
        'w_br_diff': nrm((L, BRANCH_WIDTH, D_MODEL), BRANCH_WIDTH ** -0.5),
        'w_out': nrm((L, D_MODEL, D_MODEL), D_MODEL ** -0.5),
    }


def reference(x, c, ctx, c_ctx, norm_w, w_ada, b_ada, w_in, b_merge,
              gqa_q_norm, gqa_k_norm,
              mla_q_nope_norm, mla_q_rope_norm, mla_kv_norm, mla_w_uk, mla_w_uv,
              mla_k_nope_norm, mla_k_rope_norm,
              diff_q_norm, diff_k_norm, diff_lambda_q1, diff_lambda_k1,
              diff_lambda_q2, diff_lambda_k2, diff_subln,
              w_br_gqa, w_br_mla, w_br_diff, w_out):
    n = x.shape[1]
    rows = n // GRID_W
    pos_row = jnp.repeat(jnp.arange(rows, dtype=jnp.float32), GRID_W)
    pos_col = jnp.tile(jnp.arange(GRID_W, dtype=jnp.float32), rows)
    tabs = {d: axial_rope_tables(pos_row, pos_col, d) for d in (HEAD_DIM, MLA_ROPE, DIFF_QK)}

    silu_c = jax.nn.silu(c)
    silu_cc = jax.nn.silu(c_ctx)
    for l in range(DEPTH):
        p = dict(
            w_in=w_in[l], b_merge=b_merge[l],
            gqa_q_norm=gqa_q_norm[l], gqa_k_norm=gqa_k_norm[l],
            mla_q_nope_norm=mla_q_nope_norm[l], mla_q_rope_norm=mla_q_rope_norm[l],
            mla_kv_norm=mla_kv_norm[l], mla_w_uk=mla_w_uk[l], mla_w_uv=mla_w_uv[l],
            mla_k_nope_norm=mla_k_nope_norm[l], mla_k_rope_norm=mla_k_rope_norm[l],
            diff_q_norm=diff_q_norm[l], diff_k_norm=diff_k_norm[l], diff_subln=diff_subln[l],
            w_br_gqa=w_br_gqa[l], w_br_mla=w_br_mla[l], w_br_diff=w_br_diff[l], w_out=w_out[l],
        )
        shift, scale, gate = jnp.split(silu_c @ w_ada[l] + b_ada[l], 3, axis=-1)
        shift_c, scale_c, gate_c = jnp.split(silu_cc @ w_ada[l] + b_ada[l], 3, axis=-1)
        h_lat = rms_norm(x, norm_w[l]) * (1.0 + scale[:, None, :]) + shift[:, None, :]
        h_ctx = rms_norm(ctx, norm_w[l]) * (1.0 + scale_c) + shift_c

        q_lat, kv_lat, gate_lat, merge_lat = stream_qkv(h_lat, p, tabs)
        q_ctx, kv_ctx, gate_ctx, merge_ctx = stream_qkv(h_ctx, p, None)

        lam_init = 0.8 - 0.6 * math.exp(-0.3 * l)
        lam = (jnp.exp(jnp.sum(diff_lambda_q1[l].astype(jnp.float32) * diff_lambda_k1[l].astype(jnp.float32)))
               - jnp.exp(jnp.sum(diff_lambda_q2[l].astype(jnp.float32) * diff_lambda_k2[l].astype(jnp.float32)))
               + lam_init)

        kv_all = tuple(jnp.concatenate([kc, kl], axis=1) for kc, kl in zip(kv_ctx, kv_lat))
        o_lat = sweep_query_blocks(lambda qb: attend_all(qb, kv_all, lam), q_lat)
        out_lat = merge_branches(o_lat, gate_lat, merge_lat, p, lam_init)
        if l < DEPTH - 1:
            o_ctx = attend_all(q_ctx, kv_ctx, lam)
            out_ctx = merge_branches(o_ctx, gate_ctx, merge_ctx, p, lam_init)
            ctx = ctx + gate_c * out_ctx
        x = x + gate[:, None, :] * out_lat
    return x
```

```python
import math
from contextlib import ExitStack
import numpy as np
import concourse.bass as bass
import concourse.mybir as mybir
from concourse.bass_utils import run_bass_kernel_spmd

F32 = mybir.dt.float32
BF16 = mybir.dt.bfloat16
AF = mybir.ActivationFunctionType
ALU = mybir.AluOpType
AX = mybir.AxisListType

D = 2048
KC = 16
EPS = 1e-6
GQ, GK, GV, MQ, MCKV, MKR, DQ, DK, DV, GATE, MERGE, INEND = 0, 1024, 1280, 1536, 3072, 3584, 3648, 4672, 5696, 6720, 9792, 15936
G_GQ, G_GK, G_MQN, G_MQR, G_KV, G_MKN, G_MKR, G_DQ, G_DK, G_SUB, G_END = 0, 128, 256, 384, 448, 960, 1088, 1152, 1216, 1280, 1408


class Buf:
    __slots__ = ("w", "r")

    def __init__(self):
        self.w = None
        self.r = {}


class PBuf(Buf):
    __slots__ = ("rl",)

    def __init__(self):
        Buf.__init__(self)
        self.rl = Buf()


class T:
    def __init__(self, h, psum=False):
        self.h = h
        self.b = PBuf() if psum else Buf()

    def __getitem__(self, k):
        return self.h[k]


class Sched:
    def __init__(self, nc, st):
        self.nc = nc
        self.st = st
        self.E = {"pe": nc.tensor, "act": nc.scalar, "dve": nc.vector, "pool": nc.gpsimd, "sp": nc.sync}
        self.sems = []
        self.cnt = []
        self.esem = {}
        for k in self.E:
            self.esem[k] = self.new_sem("e_" + k)
        self.seen = {k: {} for k in self.E}
        self.named = {}

    def new_sem(self, name):
        h = self.st.enter_context(self.nc.semaphore(name))
        self.sems.append(h)
        self.cnt.append(0)
        return len(self.sems) - 1

    def dsem(self, name):
        if name not in self.named:
            self.named[name] = self.new_sem("d_" + name)
        return self.named[name]

    def _waits(self, ek, reads, writes):
        deps = {}
        for b in reads:
            if b.w is not None:
                s, v = b.w
                if deps.get(s, 0) < v:
                    deps[s] = v
        for b in writes:
            if b.w is not None:
                s, v = b.w
                if deps.get(s, 0) < v:
                    deps[s] = v
            for s, v in b.r.items():
                if deps.get(s, 0) < v:
                    deps[s] = v
        seen = self.seen[ek]
        e = self.E[ek]
        for s, v in deps.items():
            if seen.get(s, 0) < v:
                e.wait_ge(self.sems[s], v)
                seen[s] = v

    def _commit(self, s, v, reads, writes):
        tok = (s, v)
        for b in writes:
            b.w = tok
            b.r = {}
        for b in reads:
            if b.r.get(s, 0) < v:
                b.r[s] = v

    def op(self, ek, fn, reads=(), writes=()):
        if ek in ("act", "dve"):
            extra = [b.rl for b in reads if isinstance(b, PBuf)]
            if extra:
                writes = list(writes) + extra
        self._waits(ek, reads, writes)
        ins = fn(self.E[ek])
        s = self.esem[ek]
        self.cnt[s] += 1
        ins.then_inc(self.sems[s], 1)
        self._commit(s, self.cnt[s], reads, writes)

    def dma(self, qk, sname, out, in_, reads=(), writes=(), **kw):
        ds = self.dsem(sname)
        self._waits(qk, reads, writes)
        ins = self.E[qk].dma_start(out=out, in_=in_, **kw)
        self.cnt[ds] += 16
        ins.then_inc(self.sems[ds], 16)
        self._commit(ds, self.cnt[ds], reads, writes)

    def barrier(self):
        for ek, e in self.E.items():
            seen = self.seen[ek]
            for s in range(len(self.sems)):
                v = self.cnt[s]
                if v > 0 and seen.get(s, 0) < v:
                    e.wait_ge(self.sems[s], v)
                    seen[s] = v


class Cfg:
    def __init__(self, L=4, SEQ=4096, CTX=256, GMAX=17, HT=9, emit_ctx=False):
        self.L, self.SEQ, self.CTX = L, SEQ, CTX
        self.emit_ctx = emit_ctx
        self.NCT = CTX // 128
        self.NLT = SEQ // 128
        self.NT = self.NCT + self.NLT
        self.NTOK = self.NT * 128
        self.GMAX = GMAX
        self.HT = HT
        tiles = list(range(self.NT))
        ng = (self.NT + GMAX - 1) // GMAX
        per = (self.NT + ng - 1) // ng
        self.groups = [tiles[i:i + per] for i in range(0, self.NT, per)]
        self.GT = per
        self.qblocks = []
        for i in range(0, self.NCT, 4):
            self.qblocks.append((list(range(i, min(i + 4, self.NCT))), list(range(self.NCT))))
        for i in range(0, self.NLT, 4):
            self.qblocks.append(([self.NCT + j for j in range(i, min(i + 4, self.NLT))], list(range(self.NT))))


def build(cfg, debug=False):
    nc = bass.Bass("TRN2", target_bir_lowering=False)
    L, SEQ, CTX, NCT, NLT, NT, NTOK = cfg.L, cfg.SEQ, cfg.CTX, cfg.NCT, cfg.NLT, cfg.NT, cfg.NTOK

    def din(name, shape, dt=F32):
        return nc.dram_tensor(name, list(shape), dt, kind="ExternalInput").ap()

    def dscr(name, shape, dt=BF16):
        return nc.dram_tensor(name, list(shape), dt, kind="ExternalOutput" if debug else "Internal").ap()

    x_in = din("x", [SEQ, D])
    ctx_in = din("ctx", [CTX, D])
    cvT = din("cvT", [128, KC, 2])
    identd = din("ident", [128, 128])
    rope128 = din("rope128", [SEQ, 256])
    rope64 = din("rope64", [SEQ, 128])
    nwT = din("nwT", [L, 128, KC])
    w_ada = din("w_ada", [L, D, 3 * D])
    b_ada = din("b_ada", [L, 3 * D])
    w_in = din("w_in", [L, D, INEND])
    b_merge = din("b_merge", [L, 3 * D])
    gains = din("gains", [L, G_END])
    lamb = din("lamb", [L, 256])
    lcst = din("lcst", [L, 2])
    w_uk = din("w_uk", [L, 512, 1024])
    w_uv = din("w_uv", [L, 512, 1024])
    w_br = din("w_br", [L, 3, 1024, D])
    w_out = din("w_out", [L, D, D])
    y_out = nc.dram_tensor("y", [SEQ, D], F32, kind="ExternalOutput").ap()
    yc_out = nc.dram_tensor("yc", [CTX, D], F32, kind="ExternalOutput").ap() if cfg.emit_ctx else None

    xs = dscr("xs", [NTOK, D], F32)
    modvec = dscr("modvec", [2, 3 * D], F32)
    KT_A = dscr("KT_A", [2, 128, NTOK])
    KT_Bn = dscr("KT_Bn", [8, 128, NTOK])
    KT_Br = dscr("KT_Br", [1, 128, NTOK])
    KT_C = dscr("KT_C", [8, 128, NTOK])
    V_A = dscr("V_A", [2, 128, NT, 128])
    V_B = dscr("V_B", [8, 128, NT, 128])
    V_C = dscr("V_C", [8, 128, NT, 128])
    QT_A = dscr("QT_A", [8, 128, NTOK])
    QT_Bn = dscr("QT_Bn", [8, 128, NTOK])
    QT_Br = dscr("QT_Br", [4, 128, NTOK])
    QT_C = dscr("QT_C", [8, 128, NTOK])
    Gd = dscr("Gd", [24, 128, NTOK])
    Md = dscr("Md", [NT, 128, 3 * D])
    BGT = dscr("BGT", [24, 128, NTOK])
    YT = dscr("YT", [NT, 128, KC, 128])

    st = ExitStack()
    st.enter_context(nc.allow_low_precision(reason="bf16 matmul operands by design; stats stay fp32"))
    S = Sched(nc, st)

    uid = [0]

    def sb(stack, name, shape, dt):
        uid[0] += 1
        return T(stack.enter_context(nc.sbuf_tensor("s%d_%s" % (uid[0], name), list(shape), dt)))

    PSF = [T(st.enter_context(nc.psum_tensor("psf%d" % i, [128, 512], F32)), psum=True) for i in range(6)]
    PSB = [T(st.enter_context(nc.psum_tensor("psb%d" % i, [128, 1024], BF16)), psum=True) for i in range(2)]
    ident = sb(st, "ident", [128, 128], F32)
    identb = sb(st, "identb", [128, 128], BF16)
    scT = sb(st, "scT", [128, KC, 2], BF16)
    cv = sb(st, "cv", [128, KC * 2], F32)
    cv2 = sb(st, "cv2", [128, KC * 2], F32)
    GN = sb(st, "GN", [128, G_END], F32)
    SUBW = sb(st, "SUBW", [128, 128], F32)
    LAM = sb(st, "LAM", [128, 8], F32)
    LB = sb(st, "LB", [128, 256], F32)
    LB2 = sb(st, "LB2", [128, 128], F32)
    AT = sb(st, "AT", [128, 2, KC], F32)
    MT = sb(st, "MT", [128, 2, 48], F32)
    NW = sb(st, "NW", [128, KC], F32)
    LC = sb(st, "LC", [128, 2], F32)
    psrot = [0]

    def next_ps():
        p = PSF[psrot[0] % 6]
        psrot[0] += 1
        return p

    pbrot = [0]

    def next_pb():
        p = PSB[pbrot[0] % 2]
        pbrot[0] += 1
        return p

    S.dma("sp", "id", ident[:, :], identd, writes=[ident.b])
    S.op("dve", lambda e: e.tensor_copy(out=identb[:, :], in_=ident[:, :]), reads=[ident.b], writes=[identb.b])
    S.dma("sp", "cv", cv[:, :], cvT.rearrange("p k s -> p (k s)"), writes=[cv.b])
    S.op("act", lambda e: e.activation(out=cv2[:, :], in_=cv[:, :], func=AF.Exp, scale=-1.0), reads=[cv.b], writes=[cv2.b])
    S.op("dve", lambda e: e.tensor_scalar(out=cv2[:, :], in0=cv2[:, :], scalar1=1.0, scalar2=None, op0=ALU.add), reads=[cv2.b], writes=[cv2.b])
    S.op("dve", lambda e: e.reciprocal(out=cv2[:, :], in_=cv2[:, :]), reads=[cv2.b], writes=[cv2.b])
    S.op("dve", lambda e: e.tensor_tensor(out=scT[:, :, :].rearrange("p k s -> p (k s)"), in0=cv[:, :], in1=cv2[:, :], op=ALU.mult),
         reads=[cv.b, cv2.b], writes=[scT.b])

    def xsrc(l, t):
        if l == 0:
            if t < NCT:
                return ctx_in[t * 128:(t + 1) * 128, :]
            return x_in[(t - NCT) * 128:(t - NCT + 1) * 128, :]
        return xs[t * 128:(t + 1) * 128, :]

    def rstd_from_ss(ss_ap, out_ap, width, ssb, outb):
        S.op("act", lambda e: e.activation(out=out_ap, in_=ss_ap, func=AF.Ln, scale=1.0 / width, bias=EPS), reads=[ssb], writes=[outb])
        S.op("act", lambda e: e.activation(out=out_ap, in_=out_ap, func=AF.Exp, scale=-0.5), reads=[outb], writes=[outb])

    for l in range(L):
        lam_init = 0.8 - 0.6 * math.exp(-0.3 * l)
        with ExitStack() as ph:
            wb = [sb(ph, "mw%d" % i, [128, KC, 512], BF16) for i in range(2)]
            brow = [sb(ph, "brow%d" % i, [1, 512], F32) for i in range(2)]
            mrow = [sb(ph, "mrow%d" % i, [1, 512], F32) for i in range(2)]
            S.dma("sp", "gn", GN[:, :], gains[l:l + 1, :].partition_broadcast(128), writes=[GN.b])
            S.dma("sp", "lb", LB[:, :], lamb[l:l + 1, :].partition_broadcast(128), writes=[LB.b])
            S.dma("sp", "nw", NW[:, :], nwT[l], writes=[NW.b])
            S.op("dve", lambda e: e.tensor_tensor(out=LB2[:, 0:64], in0=LB[:, 0:64], in1=LB[:, 64:128], op=ALU.mult), reads=[LB.b], writes=[LB2.b])
            S.op("dve", lambda e: e.tensor_tensor(out=LB2[:, 64:128], in0=LB[:, 128:192], in1=LB[:, 192:256], op=ALU.mult), reads=[LB.b, LB2.b], writes=[LB2.b])
            S.op("dve", lambda e: e.tensor_reduce(out=LAM[:, 0:2], in_=LB2[:, :].rearrange("p (a b) -> p a b", a=2), axis=AX.X, op=ALU.add), reads=[LB2.b], writes=[LAM.b])
            S.op("act", lambda e: e.activation(out=LAM[:, 2:4], in_=LAM[:, 0:2], func=AF.Exp), reads=[LAM.b], writes=[LAM.b])
            S.op("dve", lambda e: e.tensor_tensor(out=LAM[:, 4:5], in0=LAM[:, 3:4], in1=LAM[:, 2:3], op=ALU.subtract), reads=[LAM.b], writes=[LAM.b])
            S.dma("sp", "lc", LC[:, :], lcst[l:l + 1, :].partition_broadcast(128), writes=[LC.b])
            S.op("dve", lambda e: e.tensor_tensor(out=LAM[:, 5:6], in0=LAM[:, 4:5], in1=LC[:, 0:1], op=ALU.add), reads=[LAM.b, LC.b], writes=[LAM.b])
            S.op("dve", lambda e: e.tensor_scalar(out=SUBW[:, :], in0=GN[:, G_SUB:G_SUB + 128], scalar1=LC[:, 1:2], scalar2=None, op0=ALU.mult), reads=[GN.b, LC.b], writes=[SUBW.b])
            for j in range(12):
                w_ = wb[j % 2]
                S.dma("pool", "mw%d" % (j % 2), w_[:, :, :], w_ada[l, :, j * 512:(j + 1) * 512].rearrange("(k p) n -> p k n", p=128), writes=[w_.b])
                br = brow[j % 2]
                S.dma("sp", "brow%d" % (j % 2), br[:, :], b_ada[l:l + 1, j * 512:(j + 1) * 512], writes=[br.b])
                for s in range(2):
                    ps = next_ps()

                    def mm(e, ps=ps, w_=w_, s=s):
                        for k in range(KC):
                            ins = e.matmul(ps[0:1, :], lhsT=scT[:, k, s:s + 1], rhs=w_[:, k, :], start=(k == 0), stop=(k == KC - 1))
                        return ins
                    S.op("pe", mm, reads=[w_.b, scT.b], writes=[ps.b])
                    mr = mrow[s]
                    S.op("dve", lambda e, ps=ps, mr=mr, br=br: e.tensor_tensor(out=mr[:, :], in0=ps[0:1, :], in1=br[:, :], op=ALU.add), reads=[ps.b, br.b], writes=[mr.b])
                    S.dma("sp", "mrow%d" % s, modvec[s:s + 1, j * 512:(j + 1) * 512], mr[:, :], reads=[mr.b])
            S.barrier()
            for s in range(2):
                S.dma("sp", "mt%d" % s, MT[:, s, :], modvec[s, :].rearrange("(b p) -> p b", p=128), writes=[MT.b], allow_slow_non_contiguous=True)
            for s in range(2):
                S.op("dve", lambda e, s=s: e.scalar_tensor_tensor(out=AT[:, s, :], in0=MT[:, s, 16:32], scalar=1.0, in1=NW[:, :], op0=ALU.add, op1=ALU.mult),
                     reads=[MT.b, NW.b], writes=[AT.b])
            S.barrier()

        with ExitStack() as ph:
            GT = cfg.GT
            HTL = cfg.HT
            HW = HTL * 128
            hT = sb(ph, "hT", [128, KC, GT * 128], BF16)
            wbs = [sb(ph, "pw%d" % i, [128, KC, 512], BF16) for i in range(2)]
            stg = [sb(ph, "stg%d" % i, [128, 4 * HW], BF16) for i in range(3)]
            ckvT = sb(ph, "ckvT", [128, 4, GT * 128], BF16)
            XT = [sb(ph, "xt%d" % i, [128, D], F32) for i in range(2)]
            junk = sb(ph, "junk", [128, D], BF16)
            sq = sb(ph, "sq", [128, 512], F32)
            t1 = sb(ph, "t1", [128, 512], F32)
            t2 = sb(ph, "t2", [128, 512], F32)
            t3 = sb(ph, "t3", [128, 512], F32)
            ee = sb(ph, "ee", [128, 512], F32)
            qb = [sb(ph, "qb%d" % i, [128, 512], BF16) for i in range(4)]
            r128 = [sb(ph, "r128_%d" % i, [128, 256], F32) for i in range(2)]
            r64 = [sb(ph, "r64_%d" % i, [128, 128], F32) for i in range(2)]
            bm = [sb(ph, "bm%d" % i, [128, 512], F32) for i in range(2)]
            ss = sb(ph, "ss", [128, 16], F32)
            rs = sb(ph, "rs", [128, 16], F32)
            rot = {"stg": 0, "qb": 0, "r128": 0, "r64": 0, "bm": 0, "w": 0, "xt": 0}

            NSLOT = 6
            ssl = [sb(ph, "ssl%d" % i, [128, 8], F32) for i in range(NSLOT)]
            rsl = [sb(ph, "rsl%d" % i, [128, 8], F32) for i in range(NSLOT)]
            slot = [0]
            pending = []
            cur = []

            def norm_rope(src, srcb, n, w, gain, rope, ropeb, dst, dstb):
                nw_ = n * w
                k_ = slot[0] % NSLOT
                slot[0] += 1
                ss_, rs_ = ssl[k_], rsl[k_]
                sqv = sq[:, 0:nw_].rearrange("p (n w) -> p n w", n=n)
                S.op("act", lambda e: e.activation(out=sqv, in_=src, func=AF.Square), reads=[srcb], writes=[sq.b])
                S.op("dve", lambda e: e.tensor_reduce(out=ss_[:, 0:n], in_=sqv, axis=AX.X, op=ALU.add), reads=[sq.b], writes=[ss_.b])
                rstd_from_ss(ss_[:, 0:n], rs_[:, 0:n], w, ss_.b, rs_.b)

                def a2():
                    t1v = t1[:, 0:nw_].rearrange("p (n w) -> p n w", n=n)
                    S.op("dve", lambda e: e.tensor_tensor(out=t1v, in0=src, in1=rs_[:, 0:n].unsqueeze(2).to_broadcast([128, n, w]), op=ALU.mult),
                         reads=[srcb, rs_.b], writes=[t1.b])
                    gbc = gain.unsqueeze(1).to_broadcast([128, n, w])
                    if rope is None:
                        S.op("dve", lambda e: e.tensor_tensor(out=dst, in0=t1v, in1=gbc, op=ALU.mult), reads=[t1.b, GN.b], writes=[dstb])
                        return
                    S.op("dve", lambda e: e.tensor_tensor(out=t1v, in0=t1v, in1=gbc, op=ALU.mult), reads=[t1.b, GN.b], writes=[t1.b])
                    q = w // 4
                    t2v = t2[:, 0:nw_].rearrange("p (n w) -> p n w", n=n)
                    cosb = rope[:, 0:w].unsqueeze(1).to_broadcast([128, n, w])
                    S.op("dve", lambda e: e.tensor_tensor(out=t2v, in0=t1v, in1=cosb, op=ALU.mult), reads=[t1.b, ropeb], writes=[t2.b])
                    t1h = t1[:, 0:nw_].rearrange("p (n a h q) -> p n a h q", n=n, a=2, h=2)
                    t3h = t3[:, 0:nw_].rearrange("p (n a h q) -> p n a h q", n=n, a=2, h=2)
                    sinh = rope[:, w:2 * w].rearrange("p (a h q) -> p a h q", a=2, h=2)
                    for hh in range(2):
                        sv = sinh[:, :, hh, :].unsqueeze(1).to_broadcast([128, n, 2, q])
                        S.op("dve", lambda e, hh=hh, sv=sv: e.tensor_tensor(out=t3h[:, :, :, hh, :], in0=t1h[:, :, :, 1 - hh, :], in1=sv, op=ALU.mult),
                             reads=[t1.b, ropeb], writes=[t3.b])
                    t3v = t3[:, 0:nw_].rearrange("p (n w) -> p n w", n=n)
                    S.op("dve", lambda e: e.tensor_tensor(out=dst, in0=t2v, in1=t3v, op=ALU.add), reads=[t2.b, t3.b], writes=[dstb])
                cur.append(a2)

            def transposes(qbt, nblk, dst_fn, dstb):
                cur.append(lambda: transposes_now(qbt, nblk, dst_fn, dstb))

            def rotate():
                while pending:
                    pending.pop(0)()
                pending.extend(cur)
                del cur[:]

            def drain():
                rotate()
                rotate()

            def transposes_now(qbt, nblk, dst_fn, dstb):
                pb = next_pb()

                def tr(e):
                    for j in range(nblk):
                        ins = e.transpose(pb[:, j * 128:(j + 1) * 128], qbt[:, j * 128:(j + 1) * 128], identb[:, :])
                    return ins
                S.op("pe", tr, reads=[qbt.b, identb.b], writes=[pb.b])
                for j in range(nblk):
                    S.op("act", lambda e, j=j: e.activation(out=dst_fn(j), in_=pb[:, j * 128:(j + 1) * 128], func=AF.Copy), reads=[pb.b], writes=[dstb])

            for gi, gtiles in enumerate(cfg.groups):
                ng = len(gtiles)
                for ti, t in enumerate(gtiles):
                    sidx = 1 if t < NCT else 0
                    xt = XT[rot["xt"] % 2]
                    rot["xt"] += 1
                    S.dma("sp", "xt%d" % (rot["xt"] % 2), xt[:, :], xsrc(l, t), writes=[xt.b])
                    S.op("dve", lambda e, xt=xt: e.scalar_tensor_tensor(out=junk[:, :], in0=xt[:, :], scalar=1.0, in1=xt[:, :], op0=ALU.mult, op1=ALU.mult, accum_out=ss[:, 0:1]),
                         reads=[xt.b], writes=[junk.b, ss.b])
                    rstd_from_ss(ss[:, 0:1], rs[:, 0:1], D, ss.b, rs.b)
                    S.op("dve", lambda e, xt=xt: e.tensor_scalar(out=xt[:, :], in0=xt[:, :], scalar1=rs[:, 0:1], scalar2=None, op0=ALU.mult), reads=[xt.b, rs.b], writes=[xt.b])
                    for r in range(4):
                        ps = next_ps()

                        def tr(e, ps=ps, xt=xt, r=r):
                            for j in range(4):
                                k = r * 4 + j
                                ins = e.transpose(ps[:, j * 128:(j + 1) * 128], xt[:, k * 128:(k + 1) * 128], ident[:, :])
                            return ins
                        S.op("pe", tr, reads=[xt.b, ident.b], writes=[ps.b])
                        for j in range(4):
                            k = r * 4 + j
                            S.op("dve", lambda e, ps=ps, j=j, k=k, ti=ti, sidx=sidx: e.tensor_scalar(
                                out=hT[:, k, ti * 128:(ti + 1) * 128], in0=ps[:, j * 128:(j + 1) * 128],
                                scalar1=AT[:, sidx, k:k + 1], scalar2=MT[:, sidx, k:k + 1], op0=ALU.mult, op1=ALU.add),
                                reads=[ps.b, AT.b, MT.b], writes=[hT.b])

                chunks = []
                for c in range(2):
                    chunks.append(("gq", GQ + c * 512, 512, c))
                chunks.append(("gkv", GK, 512, 0))
                for c in range(4):
                    chunks.append(("mq", MQ + c * 384, 384, c))
                chunks.append(("mckv", MCKV, 512, 0))
                for c in range(2):
                    chunks.append(("uk", c * 512, 512, c))
                for c in range(2):
                    chunks.append(("uv", c * 512, 512, c))
                chunks.append(("mkr", MKR, 64, 0))
                for c in range(2):
                    chunks.append(("dq", DQ + c * 512, 512, c))
                for c in range(2):
                    chunks.append(("dk", DK + c * 512, 512, c))
                for c in range(2):
                    chunks.append(("dv", DV + c * 512, 512, c))
                for c in range(6):
                    chunks.append(("gate", GATE + c * 512, 512, c))
                for c in range(12):
                    chunks.append(("merge", MERGE + c * 512, 512, c))

                def load_w(ci):
                    kind, c0, width, c = chunks[ci]
                    w_ = wbs[ci % 2]
                    if kind == "uk":
                        src = w_uk[l, :, c0:c0 + width].rearrange("(k p) n -> p k n", p=128)
                        S.dma("pool", "pw%d" % (ci % 2), w_[:, 0:4, 0:width], src, writes=[w_.b])
                    elif kind == "uv":
                        src = w_uv[l, :, c0:c0 + width].rearrange("(k p) n -> p k n", p=128)
                        S.dma("pool", "pw%d" % (ci % 2), w_[:, 0:4, 0:width], src, writes=[w_.b])
                    else:
                        src = w_in[l, :, c0:c0 + width].rearrange("(k p) n -> p k n", p=128)
                        S.dma("pool", "pw%d" % (ci % 2), w_[:, :, 0:width], src, writes=[w_.b])

                load_w(0)
                for ci, (kind, c0, width, c) in enumerate(chunks):
                    if ci + 1 < len(chunks):
                        load_w(ci + 1)
                    w_ = wbs[ci % 2]
                    if kind == "merge":
                        bmt = bm[rot["bm"] % 2]
                        rot["bm"] += 1
                        S.dma("sp", "bm%d" % (rot["bm"] % 2), bmt[:, :], b_merge[l:l + 1, c * 512:(c + 1) * 512].partition_broadcast(128), writes=[bmt.b])
                    for h0 in range(0, ng, HTL):
                        htiles = gtiles[h0:h0 + HTL]
                        nh = len(htiles)
                        sg = stg[rot["stg"] % 3]
                        sgname = "stg%d" % (rot["stg"] % 3)
                        rot["stg"] += 1
                        sgT = sg[:, :].rearrange("p (s t) -> p s t", s=4)
                        sgV = sg[:, :].rearrange("p (s t d) -> p s t d", s=4, d=128)
                        sgM = sg[:, :].rearrange("p (t c) -> p t c", c=512)
                        for hi, t in enumerate(htiles):
                            ti = h0 + hi
                            lat = t >= NCT
                            lt = t - NCT
                            ps = next_ps()
                            lsrc = ckvT if kind in ("uk", "uv") else hT
                            nk = 4 if kind in ("uk", "uv") else KC

                            def mm(e, ps=ps, w_=w_, lsrc=lsrc, nk=nk, ti=ti, width=width):
                                for k in range(nk):
                                    ins = e.matmul(ps[:, 0:width], lhsT=lsrc[:, k, ti * 128:(ti + 1) * 128], rhs=w_[:, k, 0:width], start=(k == 0), stop=(k == nk - 1))
                                return ins
                            S.op("pe", mm, reads=[w_.b, lsrc.b], writes=[ps.b])
                            tok = slice(hi * 128, (hi + 1) * 128)
                            rp128 = rp64 = None
                            if lat and kind in ("gq", "gkv"):
                                rp128 = r128[rot["r128"] % 2]
                                rot["r128"] += 1
                                S.dma("sp", "r128_%d" % (rot["r128"] % 2), rp128[:, :], rope128[lt * 128:(lt + 1) * 128, :], writes=[rp128.b])
                            if lat and kind in ("mq", "mkr", "dq", "dk"):
                                rp64 = r64[rot["r64"] % 2]
                                rot["r64"] += 1
                                S.dma("sp", "r64_%d" % (rot["r64"] % 2), rp64[:, :], rope64[lt * 128:(lt + 1) * 128, :], writes=[rp64.b])
                            if kind == "gq":
                                q_ = qb[rot["qb"] % 4]
                                rot["qb"] += 1
                                norm_rope(ps[:, :].rearrange("p (n w) -> p n w", n=4), ps.b, 4, 128, GN[:, G_GQ:G_GQ + 128],
                                          rp128[:, :] if lat else None, rp128.b if lat else None, q_[:, :].rearrange("p (n w) -> p n w", n=4), q_.b)
                                transposes(q_, 4, lambda j, tok=tok, sgT=sgT: sgT[:, j, tok], sg.b)
                            elif kind == "gkv":
                                q_ = qb[rot["qb"] % 4]
                                rot["qb"] += 1
                                norm_rope(ps[:, 0:256].rearrange("p (n w) -> p n w", n=2), ps.b, 2, 128, GN[:, G_GK:G_GK + 128],
                                          rp128[:, :] if lat else None, rp128.b if lat else None, q_[:, 0:256].rearrange("p (n w) -> p n w", n=2), q_.b)
                                transposes(q_, 2, lambda j, tok=tok, sgT=sgT: sgT[:, j, tok], sg.b)
                                S.op("act", lambda e, ps=ps, hi=hi: e.activation(out=sgV[:, 2:4, hi, :], in_=ps[:, 256:512].rearrange("p (n w) -> p n w", n=2), func=AF.Copy),
                                     reads=[ps.b], writes=[sg.b])
                            elif kind == "mq":
                                q_ = qb[rot["qb"] % 4]
                                rot["qb"] += 1
                                pv = ps[:, 0:384].rearrange("p (n u) -> p n u", n=2)
                                norm_rope(pv[:, :, 0:128], ps.b, 2, 128, GN[:, G_MQN:G_MQN + 128], None, None,
                                          q_[:, 0:256].rearrange("p (n w) -> p n w", n=2), q_.b)
                                norm_rope(pv[:, :, 128:192], ps.b, 2, 64, GN[:, G_MQR:G_MQR + 64],
                                          rp64[:, :] if lat else None, rp64.b if lat else None, q_[:, 256:384].rearrange("p (n w) -> p n w", n=2), q_.b)
                                transposes(q_, 3, lambda j, tok=tok, sgT=sgT: sgT[:, j, tok], sg.b)
                            elif kind == "mckv":
                                q_ = qb[rot["qb"] % 4]
                                rot["qb"] += 1
                                norm_rope(ps[:, :].rearrange("p (n w) -> p n w", n=1), ps.b, 1, 512, GN[:, G_KV:G_KV + 512], None, None,
                                          q_[:, :].rearrange("p (n w) -> p n w", n=1), q_.b)
                                transposes(q_, 4, lambda j, ti=ti: ckvT[:, j, ti * 128:(ti + 1) * 128], ckvT.b)
                            elif kind == "uk":
                                q_ = qb[rot["qb"] % 4]
                                rot["qb"] += 1
                                norm_rope(ps[:, :].rearrange("p (n w) -> p n w", n=4), ps.b, 4, 128, GN[:, G_MKN:G_MKN + 128], None, None,
                                          q_[:, :].rearrange("p (n w) -> p n w", n=4), q_.b)
                                transposes(q_, 4, lambda j, tok=tok, sgT=sgT: sgT[:, j, tok], sg.b)
                            elif kind == "mkr":
                                q_ = qb[rot["qb"] % 4]
                                rot["qb"] += 1
                                norm_rope(ps[:, 0:64].rearrange("p (n w) -> p n w", n=1), ps.b, 1, 64, GN[:, G_MKR:G_MKR + 64],
                                          rp64[:, :] if lat else None, rp64.b if lat else None, q_[:, 0:64].rearrange("p (n w) -> p n w", n=1), q_.b)
                                cur.append(lambda q_=q_: S.op("dve", lambda e: e.tensor_copy(out=q_[:, 64:128], in_=q_[:, 0:64]), reads=[q_.b], writes=[q_.b]))
                                transposes(q_, 1, lambda j, tok=tok, sgT=sgT: sgT[:, j, tok], sg.b)
                            elif kind in ("dq", "dk"):
                                q_ = qb[rot["qb"] % 4]
                                rot["qb"] += 1
                                go = G_DQ if kind == "dq" else G_DK
                                norm_rope(ps[:, :].rearrange("p (n w) -> p n w", n=8), ps.b, 8, 64, GN[:, go:go + 64],
                                          rp64[:, :] if lat else None, rp64.b if lat else None, q_[:, :].rearrange("p (n w) -> p n w", n=8), q_.b)
                                transposes(q_, 4, lambda j, tok=tok, sgT=sgT: sgT[:, j, tok], sg.b)
                            elif kind in ("dv", "uv"):
                                S.op("act", lambda e, ps=ps, hi=hi: e.activation(out=sgV[:, :, hi, :], in_=ps[:, :].rearrange("p (n w) -> p n w", n=4), func=AF.Copy),
                                     reads=[ps.b], writes=[sg.b])
                            elif kind == "gate":
                                q_ = qb[rot["qb"] % 4]
                                rot["qb"] += 1
                                S.op("act", lambda e, ps=ps: e.activation(out=ee[:, :], in_=ps[:, :], func=AF.Exp, scale=-1.0), reads=[ps.b], writes=[ee.b])
                                S.op("act", lambda e: e.activation(out=ee[:, :], in_=ee[:, :], func=AF.Ln, bias=1.0), reads=[ee.b], writes=[ee.b])
                                S.op("act", lambda e: e.activation(out=ee[:, :], in_=ee[:, :], func=AF.Exp, scale=-1.0), reads=[ee.b], writes=[ee.b])
                                S.op("dve", lambda e, ps=ps, q_=q_: e.tensor_tensor(out=q_[:, :], in0=ps[:, :], in1=ee[:, :], op=ALU.mult), reads=[ps.b, ee.b], writes=[q_.b])
                                transposes(q_, 4, lambda j, tok=tok, sgT=sgT: sgT[:, j, tok], sg.b)
                            elif kind == "merge":
                                S.op("dve", lambda e, ps=ps, bmt=bmt: e.tensor_tensor(out=t1[:, :], in0=ps[:, :], in1=bmt[:, :], op=ALU.add), reads=[ps.b, bmt.b], writes=[t1.b])
                                S.op("act", lambda e: e.activation(out=ee[:, :], in_=t1[:, :], func=AF.Exp, scale=-1.0), reads=[t1.b], writes=[ee.b])
                                S.op("act", lambda e: e.activation(out=ee[:, :], in_=ee[:, :], func=AF.Ln, bias=1.0), reads=[ee.b], writes=[ee.b])
                                S.op("act", lambda e, hi=hi: e.activation(out=sgM[:, hi, :], in_=ee[:, :], func=AF.Exp, scale=-1.0), reads=[ee.b], writes=[sg.b])
                            rotate()
                        drain()
                        t0 = htiles[0]
                        tk = slice(t0 * 128, (t0 + nh) * 128)

                        def flushT(dst, h0_, ns, s0=0):
                            S.dma("sp", sgname, dst[h0_:h0_ + ns, :, tk].rearrange("h p t -> p h t"), sgT[:, s0:s0 + ns, 0:nh * 128], reads=[sg.b])

                        def flushV(dst, h0_, ns, s0=0):
                            S.dma("sp", sgname, dst[h0_:h0_ + ns, :, t0:t0 + nh, :].rearrange("h p t d -> p h t d"), sgV[:, s0:s0 + ns, 0:nh, :], reads=[sg.b])
                        if kind == "gq":
                            flushT(QT_A, 4 * c, 4)
                        elif kind == "gkv":
                            flushT(KT_A, 0, 2)
                            flushV(V_A, 0, 2, 2)
                        elif kind == "mq":
                            flushT(QT_Bn, 2 * c, 2)
                            flushT(QT_Br, c, 1, 2)
                        elif kind == "uk":
                            flushT(KT_Bn, 4 * c, 4)
                        elif kind == "uv":
                            flushV(V_B, 4 * c, 4)
                        elif kind == "mkr":
                            flushT(KT_Br, 0, 1)
                        elif kind == "dq":
                            flushT(QT_C, 4 * c, 4)
                        elif kind == "dk":
                            flushT(KT_C, 4 * c, 4)
                        elif kind == "dv":
                            flushV(V_C, 4 * c, 4)
                        elif kind == "gate":
                            flushT(Gd, 4 * c, 4)
                        elif kind == "merge":
                            S.dma("sp", sgname, Md[t0:t0 + nh, :, c * 512:(c + 1) * 512].rearrange("t p c -> p t c"), sgM[:, 0:nh, :], reads=[sg.b])
            S.barrier()

        with ExitStack() as ph:
            Kt = [sb(ph, "Kt%d" % i, [128, NTOK], BF16) for i in range(2)]
            Kz = [sb(ph, "Kz%d" % i, [128, NTOK], BF16) for i in range(2)]
            Vt = [sb(ph, "Vt%d" % i, [128, NT, 128], BF16) for i in range(2)]
            Qt = [sb(ph, "Qt%d" % i, [128, NTOK], BF16) for i in range(2)]
            Gt = [sb(ph, "Gt%d" % i, [128, NTOK], BF16) for i in range(2)]
            Kr = sb(ph, "Kr", [128, NTOK], BF16)
            Kre = [sb(ph, "Kre%d" % i, [128, NTOK], BF16) for i in range(2)]
            Qr = [sb(ph, "Qr%d" % i, [128, NTOK], BF16) for i in range(2)]
            Pt = [sb(ph, "Pt%d" % i, [128, 512], BF16) for i in range(4)]
            BGs = [sb(ph, "BGs%d" % i, [128, NTOK], BF16) for i in range(2)]
            rden = sb(ph, "rden", [128, 512], F32)
            o1 = sb(ph, "o1", [128, 512], F32)
            tt = sb(ph, "tt", [128, 512], F32)
            sqa = sb(ph, "sqa", [128, 512], F32)
            rsa = sb(ph, "rsa", [128, 512], F32)
            onesf = sb(ph, "onesf", [128, 128], F32)
            subc = sb(ph, "subc", [128, 1], F32)
            onesb = sb(ph, "onesb", [128, 128], BF16)
            S.op("pool", lambda e: e.memset(onesf[:, :], 1.0), writes=[onesf.b])
            S.op("pool", lambda e: e.memset(onesb[:, :], 1.0), writes=[onesb.b])
            for i in range(2):
                S.op("pool", lambda e, i=i: e.memset(Kz[i][:, :], 0.0), writes=[Kz[i].b])
                S.op("pool", lambda e, i=i: e.memset(Kre[i][:, :], 0.0), writes=[Kre[i].b])
            S.dma("sp", "Kr", Kr[:, :], KT_Br[0], writes=[Kr.b])
            S.op("pool", lambda e: e.tensor_copy(out=Kre[0][0:64, :], in_=Kr[0:64, :]), reads=[Kr.b], writes=[Kre[0].b])
            S.op("pool", lambda e: e.tensor_copy(out=Kre[1][64:128, :], in_=Kr[64:128, :]), reads=[Kr.b], writes=[Kre[1].b])
            S.dma("sp", "subc", subc[:, :], gains[l, G_SUB:G_SUB + 128].rearrange("(p o) -> p o", o=1), writes=[subc.b])
            S.op("dve", lambda e: e.tensor_tensor(out=subc[:, :], in0=subc[:, :], in1=LC[:, 1:2], op=ALU.mult), reads=[subc.b, LC.b], writes=[subc.b])
            prot = [0]
            SPS = PSF[0:2]
            OTS = PSF[2:4]
            DENS = PSF[4:6]
            srot = [0]
            orot = [0]
            drot = [0]
            for job in range(24):
                br_ = job // 8
                h = job % 8
                jb = job % 2
                K_, V_, Q_, G_ = Kt[jb], Vt[jb], Qt[jb], Gt[jb]
                if br_ == 0:
                    ksrc, vsrc, qsrc = KT_A[h // 4], V_A[h // 4], QT_A[h]
                    scale = 128 ** -0.5
                elif br_ == 1:
                    ksrc, vsrc, qsrc = KT_Bn[h], V_B[h], QT_Bn[h]
                    scale = 192 ** -0.5
                else:
                    ksrc, vsrc, qsrc = KT_C[h], V_C[h], QT_C[h]
                    scale = 64 ** -0.5
                S.dma("sp", "Kt%d" % jb, K_[:, :], ksrc, writes=[K_.b])
                S.dma("sp", "Vt%d" % jb, V_[:, :, :], vsrc, writes=[V_.b])
                S.dma("sp", "Qt%d" % jb, Q_[:, :], qsrc, writes=[Q_.b])
                S.dma("sp", "Gt%d" % jb, G_[:, :], Gd[job], writes=[G_.b])
                Qr_ = None
                if br_ == 1 and h % 2 == 0:
                    S.dma("sp", "Qr%d" % ((h // 2) % 2), Qr[(h // 2) % 2][:, :], QT_Br[h // 2], writes=[Qr[(h // 2) % 2].b])
                if br_ == 1:
                    Qr_ = Qr[(h // 2) % 2]
                if br_ == 2:
                    S.op("pool", lambda e, K_=K_: e.tensor_copy(out=Kz[0][0:64, :], in_=K_[0:64, :]), reads=[K_.b], writes=[Kz[0].b])
                    S.op("pool", lambda e, K_=K_: e.tensor_copy(out=Kz[1][64:128, :], in_=K_[64:128, :]), reads=[K_.b], writes=[Kz[1].b])
                bgs = BGs[jb]
                npass = 2 if br_ == 2 else 1
                for (qtiles, kchunks) in cfg.qblocks:
                    nq = len(qtiles)
                    qs = slice(qtiles[0] * 128, (qtiles[0] + nq) * 128)
                    qw = nq * 128
                    for pz in range(npass):
                        nkc = len(kchunks)
                        OT = OTS[orot[0] % 2]
                        orot[0] += 1
                        DENP = DENS[drot[0] % 2]
                        SSP = DENP
                        drot[0] += 1

                        def emit_S(ci):
                            kc_ = kchunks[ci]
                            sp_ = SPS[srot[0] % 2]
                            srot[0] += 1
                            ks = slice(kc_ * 128, (kc_ + 1) * 128)

                            def mm(e):
                                if br_ == 0:
                                    return e.matmul(sp_[:, 0:qw], lhsT=K_[:, ks], rhs=Q_[:, qs], start=True, stop=True)
                                if br_ == 1:
                                    e.matmul(sp_[:, 0:qw], lhsT=K_[:, ks], rhs=Q_[:, qs], start=True, stop=False)
                                    return e.matmul(sp_[:, 0:qw], lhsT=Kre[h % 2][:, ks], rhs=Qr_[:, qs], start=False, stop=True)
                                return e.matmul(sp_[:, 0:qw], lhsT=Kz[pz][:, ks], rhs=Q_[:, qs], start=True, stop=True)
                            if br_ == 0:
                                rd = [K_.b, Q_.b]
                            elif br_ == 1:
                                rd = [K_.b, Q_.b, Kre[h % 2].b, Qr_.b]
                            else:
                                rd = [Kz[pz].b, Q_.b]
                            S.op("pe", mm, reads=rd, writes=[sp_.b])
                            return sp_

                        pendq = [emit_S(0)]
                        for ci in range(nkc):
                            sp_ = pendq.pop(0)
                            if ci + 1 < nkc:
                                pendq.append(emit_S(ci + 1))
                            p_ = Pt[prot[0] % 4]
                            prot[0] += 1
                            S.op("act", lambda e, sp_=sp_, p_=p_: e.activation(out=p_[:, 0:qw], in_=sp_[:, 0:qw], func=AF.Exp, scale=scale), reads=[sp_.b], writes=[p_.b])
                            kc_ = kchunks[ci]
                            wr = [OT.b, DENP.b] if (ci == 0 or ci == nkc - 1) else []

                            def pvden(e, p_=p_, kc_=kc_, ci=ci, OT=OT, DENP=DENP):
                                e.matmul(OT[:, 0:qw], lhsT=V_[:, kc_, :], rhs=p_[:, 0:qw], start=(ci == 0), stop=(ci == nkc - 1))
                                return e.matmul(DENP[:, 0:qw], lhsT=onesb[:, :], rhs=p_[:, 0:qw], start=(ci == 0), stop=(ci == nkc - 1))
                            S.op("pe", pvden, reads=[p_.b, V_.b, onesb.b], writes=wr)
                        S.op("dve", lambda e, DENP=DENP: e.reciprocal(out=rden[:, 0:qw], in_=DENP[:, 0:qw]), reads=[DENP.b], writes=[rden.b])
                        if br_ < 2:
                            S.op("dve", lambda e, OT=OT: e.tensor_tensor(out=tt[:, 0:qw], in0=OT[:, 0:qw], in1=rden[:, 0:qw], op=ALU.mult), reads=[OT.b, rden.b], writes=[tt.b])
                            S.op("dve", lambda e: e.tensor_tensor(out=bgs[:, qs], in0=tt[:, 0:qw], in1=G_[:, qs], op=ALU.mult), reads=[tt.b, G_.b], writes=[bgs.b])
                        elif pz == 0:
                            S.op("dve", lambda e, OT=OT: e.tensor_tensor(out=o1[:, 0:qw], in0=OT[:, 0:qw], in1=rden[:, 0:qw], op=ALU.mult), reads=[OT.b, rden.b], writes=[o1.b])
                        else:
                            S.op("dve", lambda e, OT=OT: e.tensor_tensor(out=tt[:, 0:qw], in0=OT[:, 0:qw], in1=rden[:, 0:qw], op=ALU.mult), reads=[OT.b, rden.b], writes=[tt.b])
                            S.op("dve", lambda e: e.scalar_tensor_tensor(out=tt[:, 0:qw], in0=tt[:, 0:qw], scalar=LAM[:, 5:6], in1=o1[:, 0:qw], op0=ALU.mult, op1=ALU.add),
                                 reads=[tt.b, LAM.b, o1.b], writes=[tt.b])
                            S.op("act", lambda e: e.activation(out=sqa[:, 0:qw], in_=tt[:, 0:qw], func=AF.Square), reads=[tt.b], writes=[sqa.b])
                            S.op("pe", lambda e, SSP=SSP: e.matmul(SSP[:, 0:qw], lhsT=onesf[:, :], rhs=sqa[:, 0:qw], start=True, stop=True), reads=[sqa.b, onesf.b], writes=[SSP.b])
                            S.op("act", lambda e, SSP=SSP: e.activation(out=rsa[:, 0:qw], in_=SSP[:, 0:qw], func=AF.Ln, scale=1.0 / 128, bias=EPS), reads=[SSP.b], writes=[rsa.b])
                            S.op("act", lambda e: e.activation(out=rsa[:, 0:qw], in_=rsa[:, 0:qw], func=AF.Exp, scale=-0.5), reads=[rsa.b], writes=[rsa.b])
                            S.op("dve", lambda e: e.tensor_tensor(out=tt[:, 0:qw], in0=tt[:, 0:qw], in1=rsa[:, 0:qw], op=ALU.mult), reads=[tt.b, rsa.b], writes=[tt.b])
                            S.op("dve", lambda e: e.scalar_tensor_tensor(out=bgs[:, qs], in0=tt[:, 0:qw], scalar=subc[:, 0:1], in1=G_[:, qs], op0=ALU.mult, op1=ALU.mult),
                                 reads=[tt.b, subc.b, G_.b], writes=[bgs.b])
                S.dma("sp", "BGs%d" % jb, BGT[job], bgs[:, :], reads=[bgs.b])
            S.barrier()

        with ExitStack() as ph:
            Wb = [sb(ph, "Wb%d" % i, [128, 8, D], BF16) for i in range(3)]
            bgt = [sb(ph, "bgt%d" % i, [128, 24, 128], BF16) for i in range(2)]
            mt = [sb(ph, "mt%d" % i, [128, 3 * D], BF16) for i in range(2)]
            yf = sb(ph, "yf", [128, D], F32)
            tm = sb(ph, "tm", [128, 512], F32)
            ybf = sb(ph, "ybf", [128, D], BF16)
            yTt = [sb(ph, "yTt%d" % i, [128, KC, 128], BF16) for i in range(2)]
            for i in range(3):
                for hh in range(2):
                    S.dma("pool", "Wb%d_%d" % (i, hh), Wb[i][:, hh * 4:(hh + 1) * 4, :], w_br[l, i, hh * 512:(hh + 1) * 512, :].rearrange("(h p) n -> p h n", p=128), writes=[Wb[i].b])
            for t in range(NT):
                b_ = bgt[t % 2]
                m_ = mt[t % 2]
                S.dma("sp", "bgt%d" % (t % 2), b_[:, :, :], BGT[:, :, t * 128:(t + 1) * 128].rearrange("h p t -> p h t"), writes=[b_.b])
                S.dma("sp", "mt%d" % (t % 2), m_[:, :], Md[t], writes=[m_.b])
                for n in range(4):
                    ns = slice(n * 512, (n + 1) * 512)
                    for i in range(3):
                        ps = next_ps()

                        def mm(e, ps=ps, i=i, b_=b_, ns=ns):
                            for hh in range(8):
                                ins = e.matmul(ps[:, :], lhsT=b_[:, 8 * i + hh, :], rhs=Wb[i][:, hh, ns], start=(hh == 0), stop=(hh == 7))
                            return ins
                        S.op("pe", mm, reads=[b_.b, Wb[i].b], writes=[ps.b])
                        mi = m_[:, i * D + n * 512:i * D + (n + 1) * 512]
                        if i == 0:
                            S.op("dve", lambda e, ps=ps, mi=mi, ns=ns: e.tensor_tensor(out=yf[:, ns], in0=ps[:, :], in1=mi, op=ALU.mult), reads=[ps.b, m_.b], writes=[yf.b])
                        else:
                            S.op("dve", lambda e, ps=ps, mi=mi: e.tensor_tensor(out=tm[:, :], in0=ps[:, :], in1=mi, op=ALU.mult), reads=[ps.b, m_.b], writes=[tm.b])
                            if i == 1:
                                S.op("dve", lambda e, ns=ns: e.tensor_tensor(out=yf[:, ns], in0=yf[:, ns], in1=tm[:, :], op=ALU.add), reads=[yf.b, tm.b], writes=[yf.b])
                            else:
                                S.op("dve", lambda e, ns=ns: e.tensor_tensor(out=ybf[:, ns], in0=yf[:, ns], in1=tm[:, :], op=ALU.add), reads=[yf.b, tm.b], writes=[ybf.b])
                yt_ = yTt[t % 2]
                for r in range(2):
                    pb = next_pb()

                    def tr(e, pb=pb, r=r):
                        for j in range(8):
                            k = r * 8 + j
                            ins = e.transpose(pb[:, j * 128:(j + 1) * 128], ybf[:, k * 128:(k + 1) * 128], identb[:, :])
                        return ins
                    S.op("pe", tr, reads=[ybf.b, identb.b], writes=[pb.b])
                    S.op("act", lambda e, pb=pb, r=r, yt_=yt_: e.activation(out=yt_[:, r * 8:(r + 1) * 8, :], in_=pb[:, :].rearrange("p (k t) -> p k t", k=8), func=AF.Copy),
                         reads=[pb.b], writes=[yt_.b])
                S.dma("sp", "yTt%d" % (t % 2), YT[t], yt_[:, :, :], reads=[yt_.b])
            S.barrier()

        with ExitStack() as ph:
            Wo = sb(ph, "Wo", [128, KC, D], BF16)
            gbc = sb(ph, "gbc", [128, 2, D], F32)
            yTt = [sb(ph, "yTb%d" % i, [128, KC, 128], BF16) for i in range(2)]
            xt2 = [sb(ph, "xb%d" % i, [128, D], F32) for i in range(2)]
            xo = [sb(ph, "xo%d" % i, [128, D], F32) for i in range(2)]
            tm = sb(ph, "tmb", [128, 512], F32)
            for q4 in range(4):
                S.dma("pool", "Wo%d" % q4, Wo[:, q4 * 4:(q4 + 1) * 4, :], w_out[l, q4 * 512:(q4 + 1) * 512, :].rearrange("(k p) n -> p k n", p=128), writes=[Wo.b])
            for s in range(2):
                S.dma("sp", "gbc%d" % s, gbc[:, s, :], modvec[s:s + 1, 2 * D:3 * D].partition_broadcast(128), writes=[gbc.b])
            last = (l == L - 1)
            for t in range(NT):
                if last and t < NCT and not cfg.emit_ctx:
                    continue
                sidx = 1 if t < NCT else 0
                yt_ = yTt[t % 2]
                x_ = xt2[t % 2]
                o_ = xo[t % 2]
                S.dma("sp", "yTb%d" % (t % 2), yt_[:, :, :], YT[t], writes=[yt_.b])
                S.dma("sp", "xb%d" % (t % 2), x_[:, :], xsrc(l, t), writes=[x_.b])
                for n in range(4):
                    ns = slice(n * 512, (n + 1) * 512)
                    ps = next_ps()

                    def mm(e, ps=ps, yt_=yt_, ns=ns):
                        for k in range(KC):
                            ins = e.matmul(ps[:, :], lhsT=yt_[:, k, :], rhs=Wo[:, k, ns], start=(k == 0), stop=(k == KC - 1))
                        return ins
                    S.op("pe", mm, reads=[yt_.b, Wo.b], writes=[ps.b])
                    S.op("dve", lambda e, ps=ps, ns=ns, sidx=sidx: e.tensor_tensor(out=tm[:, :], in0=ps[:, :], in1=gbc[:, sidx, ns], op=ALU.mult), reads=[ps.b, gbc.b], writes=[tm.b])
                    S.op("dve", lambda e, ns=ns, x_=x_, o_=o_: e.tensor_tensor(out=o_[:, ns], in0=x_[:, ns], in1=tm[:, :], op=ALU.add), reads=[x_.b, tm.b], writes=[o_.b])
                if last and t < NCT:
                    dst = yc_out[t * 128:(t + 1) * 128, :]
                elif last:
                    dst = y_out[(t - NCT) * 128:(t - NCT + 1) * 128, :]
                else:
                    dst = xs[t * 128:(t + 1) * 128, :]
                S.dma("sp", "xo%d" % (t % 2), dst, o_[:, :], reads=[o_.b])
            S.barrier()
    st.close()
    return nc


def rope_tables(SEQ, grid_w=64, theta=10000.0):
    t = np.arange(SEQ)
    pos_row = (t // grid_w).astype(np.float32)
    pos_col = (t % grid_w).astype(np.float32)
    out = {}
    for d in (128, 64):
        axis_dim = d // 2
        inv_freq = (np.float32(theta) ** (-np.arange(0, axis_dim, 2, dtype=np.float32) / np.float32(axis_dim))).astype(np.float32)
        ang_r = pos_row[:, None] * inv_freq
        ang_c = pos_col[:, None] * inv_freq
        ang = np.concatenate([ang_r, ang_r, ang_c, ang_c], axis=-1).astype(np.float32)
        cos = np.cos(ang).astype(np.float32)
        sin = np.sin(ang).astype(np.float32)
        q = d // 4
        sign = np.concatenate([-np.ones(q), np.ones(q), -np.ones(q), np.ones(q)]).astype(np.float32)
        out[d] = np.ascontiguousarray(np.concatenate([cos, sin * sign], axis=-1))
    return out


def make_in_maps(inp, cfg, nb, l0=0, x_cur=None, ctx_cur=None):
    L = cfg.L

    def f(a):
        a = np.asarray(a, dtype=np.float32)
        return np.ascontiguousarray(a)
    inp = dict(inp)
    for k in list(inp.keys()):
        if k not in ("x", "c", "ctx", "c_ctx"):
            inp[k] = np.asarray(inp[k])[l0:l0 + L]
    lcst = np.array([[-(0.8 - 0.6 * math.exp(-0.3 * l)), 1.0 - (0.8 - 0.6 * math.exp(-0.3 * l))] for l in range(l0, l0 + L)], dtype=np.float32)
    tabs = rope_tables(cfg.SEQ)
    gains = np.concatenate([f(inp[k]) for k in ("gqa_q_norm", "gqa_k_norm", "mla_q_nope_norm", "mla_q_rope_norm", "mla_kv_norm",
                                                "mla_k_nope_norm", "mla_k_rope_norm", "diff_q_norm", "diff_k_norm", "diff_subln")], axis=1)
    lamb = np.concatenate([f(inp[k]) for k in ("diff_lambda_q1", "diff_lambda_k1", "diff_lambda_q2", "diff_lambda_k2")], axis=1)
    shared = {
        "ident": np.eye(128, dtype=np.float32),
        "rope128": tabs[128], "rope64": tabs[64],
        "nwT": np.ascontiguousarray(f(inp["norm_w"]).reshape(L, KC, 128).transpose(0, 2, 1)),
        "w_ada": f(inp["w_ada"]), "b_ada": f(inp["b_ada"]), "w_in": f(inp["w_in"]), "b_merge": f(inp["b_merge"]),
        "gains": np.ascontiguousarray(gains), "lamb": np.ascontiguousarray(lamb), "lcst": lcst,
        "w_uk": f(inp["mla_w_uk"]), "w_uv": f(inp["mla_w_uv"]),
        "w_br": np.ascontiguousarray(np.stack([f(inp["w_br_gqa"]), f(inp["w_br_mla"]), f(inp["w_br_diff"])], axis=1)),
        "w_out": f(inp["w_out"]),
    }
    x = f(inp["x"]) if x_cur is None else x_cur
    ctx = f(inp["ctx"]) if ctx_cur is None else ctx_cur
    c = f(inp["c"])
    cc = f(inp["c_ctx"])
    maps = []
    for b in range(nb):
        cv = np.stack([c[b], cc], axis=-1)
        cvT = np.ascontiguousarray(cv.reshape(KC, 128, 2).transpose(1, 0, 2))
        m = dict(shared)
        m.update({"x": x[b], "ctx": ctx[b], "cvT": cvT})
        maps.append(m)
    return maps


def kernel(**inputs):
    cfg = Cfg(L=4)
    nc = build(cfg)
    maps = make_in_maps(inputs, cfg, 4)
    res = run_bass_kernel_spmd(nc, [maps[i] for i in range(4)], core_ids=list(range(4)))
    return np.stack([res.results[b]["y"] for b in range(4)], axis=0).astype(np.float32)
```

```python
import math
from contextlib import ExitStack
import numpy as np
import concourse.bass as bass
import concourse.mybir as mybir
from concourse.bass_utils import run_bass_kernel_spmd

F32 = mybir.dt.float32
BF16 = mybir.dt.bfloat16
AF = mybir.ActivationFunctionType
ALU = mybir.AluOpType
AX = mybir.AxisListType

D = 2048
KC = 16
EPS = 1e-6
GQ, GK, GV, MQ, MCKV, MKR, DQ, DK, DV, GATE, MERGE, INEND = 0, 1024, 1280, 1536, 3072, 3584, 3648, 4672, 5696, 6720, 9792, 15936
G_GQ, G_GK, G_MQN, G_MQR, G_KV, G_MKN, G_MKR, G_DQ, G_DK, G_SUB, G_END = 0, 128, 256, 384, 448, 960, 1088, 1152, 1216, 1280, 1408


class Buf:
    __slots__ = ("w", "r")

    def __init__(self):
        self.w = None
        self.r = {}


class PBuf(Buf):
    __slots__ = ("rl",)

    def __init__(self):
        Buf.__init__(self)
        self.rl = Buf()


class T:
    def __init__(self, h, psum=False):
        self.h = h
        self.b = PBuf() if psum else Buf()

    def __getitem__(self, k):
        return self.h[k]


class Sched:
    def __init__(self, nc, st):
        self.nc = nc
        self.st = st
        self.E = {"pe": nc.tensor, "act": nc.scalar, "dve": nc.vector, "pool": nc.gpsimd, "sp": nc.sync}
        self.sems = []
        self.cnt = []
        self.esem = {}
        for k in self.E:
            self.esem[k] = self.new_sem("e_" + k)
        self.seen = {k: {} for k in self.E}
        self.named = {}

    def new_sem(self, name):
        h = self.st.enter_context(self.nc.semaphore(name))
        self.sems.append(h)
        self.cnt.append(0)
        return len(self.sems) - 1

    def dsem(self, name):
        if name not in self.named:
            self.named[name] = self.new_sem("d_" + name)
        return self.named[name]

    def _waits(self, ek, reads, writes):
        deps = {}
        for b in reads:
            if b.w is not None:
                s, v = b.w
                if deps.get(s, 0) < v:
                    deps[s] = v
        for b in writes:
            if b.w is not None:
                s, v = b.w
                if deps.get(s, 0) < v:
                    deps[s] = v
            for s, v in b.r.items():
                if deps.get(s, 0) < v:
                    deps[s] = v
        seen = self.seen[ek]
        e = self.E[ek]
        for s, v in deps.items():
            if seen.get(s, 0) < v:
                e.wait_ge(self.sems[s], v)
                seen[s] = v

    def _commit(self, s, v, reads, writes):
        tok = (s, v)
        for b in writes:
            b.w = tok
            b.r = {}
        for b in reads:
            if b.r.get(s, 0) < v:
                b.r[s] = v

    def op(self, ek, fn, reads=(), writes=()):
        if ek in ("act", "dve"):
            extra = [b.rl for b in reads if isinstance(b, PBuf)]
            if extra:
                writes = list(writes) + extra
        self._waits(ek, reads, writes)
        ins = fn(self.E[ek])
        s = self.esem[ek]
        self.cnt[s] += 1
        ins.then_inc(self.sems[s], 1)
        self._commit(s, self.cnt[s], reads, writes)

    def dma(self, qk, sname, out, in_, reads=(), writes=(), **kw):
        ds = self.dsem(sname)
        self._waits(qk, reads, writes)
        ins = self.E[qk].dma_start(out=out, in_=in_, **kw)
        self.cnt[ds] += 16
        ins.then_inc(self.sems[ds], 16)
        self._commit(ds, self.cnt[ds], reads, writes)

    def barrier(self):
        for ek, e in self.E.items():
            seen = self.seen[ek]
            for s in range(len(self.sems)):
                v = self.cnt[s]
                if v > 0 and seen.get(s, 0) < v:
                    e.wait_ge(self.sems[s], v)
                    seen[s] = v


class Cfg:
    def __init__(self, L=4, SEQ=4096, CTX=256, GMAX=17, HT=9, emit_ctx=False):
        self.L, self.SEQ, self.CTX = L, SEQ, CTX
        self.emit_ctx = emit_ctx
        self.NCT = CTX // 128
        self.NLT = SEQ // 128
        self.NT = self.NCT + self.NLT
        self.NTOK = self.NT * 128
        self.GMAX = GMAX
        self.HT = HT
        tiles = list(range(self.NT))
        ng = (self.NT + GMAX - 1) // GMAX
        per = (self.NT + ng - 1) // ng
        self.groups = [tiles[i:i + per] for i in range(0, self.NT, per)]
        self.GT = per
        self.qblocks = []
        for i in range(0, self.NCT, 4):
            self.qblocks.append((list(range(i, min(i + 4, self.NCT))), list(range(self.NCT))))
        for i in range(0, self.NLT, 4):
            self.qblocks.append(([self.NCT + j for j in range(i, min(i + 4, self.NLT))], list(range(self.NT))))


def build(cfg, debug=False):
    nc = bass.Bass("TRN2", target_bir_lowering=False)
    L, SEQ, CTX, NCT, NLT, NT, NTOK = cfg.L, cfg.SEQ, cfg.CTX, cfg.NCT, cfg.NLT, cfg.NT, cfg.NTOK

    def din(name, shape, dt=F32):
        return nc.dram_tensor(name, list(shape), dt, kind="ExternalInput").ap()

    def dscr(name, shape, dt=BF16):
        return nc.dram_tensor(name, list(shape), dt, kind="ExternalOutput" if debug else "Internal").ap()

    x_in = din("x", [SEQ, D])
    ctx_in = din("ctx", [CTX, D])
    cvT = din("cvT", [128, KC, 2])
    identd = din("ident", [128, 128])
    rope128 = din("rope128", [SEQ, 256])
    rope64 = din("rope64", [SEQ, 128])
    nwT = din("nwT", [L, 128, KC])
    w_ada = din("w_ada", [L, D, 3 * D])
    b_ada = din("b_ada", [L, 3 * D])
    w_in = din("w_in", [L, D, INEND])
    b_merge = din("b_merge", [L, 3 * D])
    gains = din("gains", [L, G_END])
    lamb = din("lamb", [L, 256])
    lcst = din("lcst", [L, 2])
    w_uk = din("w_uk", [L, 512, 1024])
    w_uv = din("w_uv", [L, 512, 1024])
    w_br = din("w_br", [L, 3, 1024, D])
    w_out = din("w_out", [L, D, D])
    y_out = nc.dram_tensor("y", [SEQ, D], F32, kind="ExternalOutput").ap()
    yc_out = nc.dram_tensor("yc", [CTX, D], F32, kind="ExternalOutput").ap() if cfg.emit_ctx else None

    xs = dscr("xs", [NTOK, D], F32)
    modvec = dscr("modvec", [2, 3 * D], F32)
    KT_A = dscr("KT_A", [2, 128, NTOK])
    KT_Bn = dscr("KT_Bn", [8, 128, NTOK])
    KT_Br = dscr("KT_Br", [1, 128, NTOK])
    KT_C = dscr("KT_C", [8, 128, NTOK])
    V_A = dscr("V_A", [2, 128, NT, 128])
    V_B = dscr("V_B", [8, 128, NT, 128])
    V_C = dscr("V_C", [8, 128, NT, 128])
    QT_A = dscr("QT_A", [8, 128, NTOK])
    QT_Bn = dscr("QT_Bn", [8, 128, NTOK])
    QT_Br = dscr("QT_Br", [4, 128, NTOK])
    QT_C = dscr("QT_C", [8, 128, NTOK])
    Gd = dscr("Gd", [24, 128, NTOK])
    Md = dscr("Md", [NT, 128, 3 * D])
    BGT = dscr("BGT", [24, 128, NTOK])
    YT = dscr("YT", [NT, 128, KC, 128])

    st = ExitStack()
    st.enter_context(nc.allow_low_precision(reason="bf16 matmul operands by design; stats stay fp32"))
    S = Sched(nc, st)

    uid = [0]

    def sb(stack, name, shape, dt):
        uid[0] += 1
        return T(stack.enter_context(nc.sbuf_tensor("s%d_%s" % (uid[0], name), list(shape), dt)))

    PSF = [T(st.enter_context(nc.psum_tensor("psf%d" % i, [128, 512], F32)), psum=True) for i in range(6)]
    PSB = []
    ident = sb(st, "ident", [128, 128], F32)
    identb = sb(st, "identb", [128, 128], BF16)
    scT = sb(st, "scT", [128, KC, 2], BF16)
    cv = sb(st, "cv", [128, KC * 2], F32)
    cv2 = sb(st, "cv2", [128, KC * 2], F32)
    GN = sb(st, "GN", [128, G_END], F32)
    SUBW = sb(st, "SUBW", [128, 128], F32)
    LAM = sb(st, "LAM", [128, 8], F32)
    LB = sb(st, "LB", [128, 256], F32)
    LB2 = sb(st, "LB2", [128, 128], F32)
    AT = sb(st, "AT", [128, 2, KC], F32)
    MT = sb(st, "MT", [128, 2, 48], F32)
    NW = sb(st, "NW", [128, KC], F32)
    LC = sb(st, "LC", [128, 2], F32)
    psrot = [0]

    def next_ps():
        p = PSF[psrot[0] % 6]
        psrot[0] += 1
        return p

    pbrot = [0]

    def next_pb():
        p = PSB[pbrot[0] % 2]
        pbrot[0] += 1
        return p

    S.dma("sp", "id", ident[:, :], identd, writes=[ident.b])
    S.op("dve", lambda e: e.tensor_copy(out=identb[:, :], in_=ident[:, :]), reads=[ident.b], writes=[identb.b])
    S.dma("sp", "cv", cv[:, :], cvT.rearrange("p k s -> p (k s)"), writes=[cv.b])
    S.op("act", lambda e: e.activation(out=cv2[:, :], in_=cv[:, :], func=AF.Exp, scale=-1.0), reads=[cv.b], writes=[cv2.b])
    S.op("dve", lambda e: e.tensor_scalar(out=cv2[:, :], in0=cv2[:, :], scalar1=1.0, scalar2=None, op0=ALU.add), reads=[cv2.b], writes=[cv2.b])
    S.op("dve", lambda e: e.reciprocal(out=cv2[:, :], in_=cv2[:, :]), reads=[cv2.b], writes=[cv2.b])
    S.op("dve", lambda e: e.tensor_tensor(out=scT[:, :, :].rearrange("p k s -> p (k s)"), in0=cv[:, :], in1=cv2[:, :], op=ALU.mult),
         reads=[cv.b, cv2.b], writes=[scT.b])

    def xsrc(l, t):
        if l == 0:
            if t < NCT:
                return ctx_in[t * 128:(t + 1) * 128, :]
            return x_in[(t - NCT) * 128:(t - NCT + 1) * 128, :]
        return xs[t * 128:(t + 1) * 128, :]

    def rstd_from_ss(ss_ap, out_ap, width, ssb, outb):
        S.op("act", lambda e: e.activation(out=out_ap, in_=ss_ap, func=AF.Ln, scale=1.0 / width, bias=EPS), reads=[ssb], writes=[outb])
        S.op("act", lambda e: e.activation(out=out_ap, in_=out_ap, func=AF.Exp, scale=-0.5), reads=[outb], writes=[outb])

    for l in range(L):
        lam_init = 0.8 - 0.6 * math.exp(-0.3 * l)
        with ExitStack() as ph:
            wb = [sb(ph, "mw%d" % i, [128, KC, 512], BF16) for i in range(2)]
            brow = [sb(ph, "brow%d" % i, [1, 512], F32) for i in range(2)]
            mrow = [sb(ph, "mrow%d" % i, [1, 512], F32) for i in range(2)]
            S.dma("sp", "gn", GN[:, :], gains[l:l + 1, :].partition_broadcast(128), writes=[GN.b])
            S.dma("sp", "lb", LB[:, :], lamb[l:l + 1, :].partition_broadcast(128), writes=[LB.b])
            S.dma("sp", "nw", NW[:, :], nwT[l], writes=[NW.b])
            S.op("dve", lambda e: e.tensor_tensor(out=LB2[:, 0:64], in0=LB[:, 0:64], in1=LB[:, 64:128], op=ALU.mult), reads=[LB.b], writes=[LB2.b])
            S.op("dve", lambda e: e.tensor_tensor(out=LB2[:, 64:128], in0=LB[:, 128:192], in1=LB[:, 192:256], op=ALU.mult), reads=[LB.b, LB2.b], writes=[LB2.b])
            S.op("dve", lambda e: e.tensor_reduce(out=LAM[:, 0:2], in_=LB2[:, :].rearrange("p (a b) -> p a b", a=2), axis=AX.X, op=ALU.add), reads=[LB2.b], writes=[LAM.b])
            S.op("act", lambda e: e.activation(out=LAM[:, 2:4], in_=LAM[:, 0:2], func=AF.Exp), reads=[LAM.b], writes=[LAM.b])
            S.op("dve", lambda e: e.tensor_tensor(out=LAM[:, 4:5], in0=LAM[:, 3:4], in1=LAM[:, 2:3], op=ALU.subtract), reads=[LAM.b], writes=[LAM.b])
            S.dma("sp", "lc", LC[:, :], lcst[l:l + 1, :].partition_broadcast(128), writes=[LC.b])
            S.op("dve", lambda e: e.tensor_tensor(out=LAM[:, 5:6], in0=LAM[:, 4:5], in1=LC[:, 0:1], op=ALU.add), reads=[LAM.b, LC.b], writes=[LAM.b])
            S.op("dve", lambda e: e.tensor_scalar(out=SUBW[:, :], in0=GN[:, G_SUB:G_SUB + 128], scalar1=LC[:, 1:2], scalar2=None, op0=ALU.mult), reads=[GN.b, LC.b], writes=[SUBW.b])
            for j in range(12):
                w_ = wb[j % 2]
                S.dma("pool", "mw%d" % (j % 2), w_[:, :, :], w_ada[l, :, j * 512:(j + 1) * 512].rearrange("(k p) n -> p k n", p=128), writes=[w_.b])
                br = brow[j % 2]
                S.dma("sp", "brow%d" % (j % 2), br[:, :], b_ada[l:l + 1, j * 512:(j + 1) * 512], writes=[br.b])
                for s in range(2):
                    ps = next_ps()

                    def mm(e, ps=ps, w_=w_, s=s):
                        for k in range(KC):
                            ins = e.matmul(ps[0:1, :], lhsT=scT[:, k, s:s + 1], rhs=w_[:, k, :], start=(k == 0), stop=(k == KC - 1))
                        return ins
                    S.op("pe", mm, reads=[w_.b, scT.b], writes=[ps.b])
                    mr = mrow[s]
                    S.op("dve", lambda e, ps=ps, mr=mr, br=br: e.tensor_tensor(out=mr[:, :], in0=ps[0:1, :], in1=br[:, :], op=ALU.add), reads=[ps.b, br.b], writes=[mr.b])
                    S.dma("sp", "mrow%d" % s, modvec[s:s + 1, j * 512:(j + 1) * 512], mr[:, :], reads=[mr.b])
            S.barrier()
            for s in range(2):
                S.dma("sp", "mt%d" % s, MT[:, s, :], modvec[s, :].rearrange("(b p) -> p b", p=128), writes=[MT.b], allow_slow_non_contiguous=True)
            for s in range(2):
                S.op("dve", lambda e, s=s: e.scalar_tensor_tensor(out=AT[:, s, :], in0=MT[:, s, 16:32], scalar=1.0, in1=NW[:, :], op0=ALU.add, op1=ALU.mult),
                     reads=[MT.b, NW.b], writes=[AT.b])
            S.barrier()

        with ExitStack() as ph:
            uid[0] += 1
            PSB[:] = [T(ph.enter_context(nc.psum_tensor("psb%d_%d" % (uid[0], i), [128, 1024], BF16)), psum=True) for i in range(2)]
            GT = cfg.GT
            HTL = cfg.HT
            HW = HTL * 128
            hT = sb(ph, "hT", [128, KC, GT * 128], BF16)
            wbs = [sb(ph, "pw%d" % i, [128, KC, 512], BF16) for i in range(2)]
            stg = [sb(ph, "stg%d" % i, [128, 4 * HW], BF16) for i in range(3)]
            ckvT = sb(ph, "ckvT", [128, 4, GT * 128], BF16)
            XT = [sb(ph, "xt%d" % i, [128, D], F32) for i in range(2)]
            junk = sb(ph, "junk", [128, D], BF16)
            sq = sb(ph, "sq", [128, 512], F32)
            t1 = sb(ph, "t1", [128, 512], F32)
            t2 = sb(ph, "t2", [128, 512], F32)
            t3 = sb(ph, "t3", [128, 512], F32)
            ee = sb(ph, "ee", [128, 512], F32)
            qb = [sb(ph, "qb%d" % i, [128, 512], BF16) for i in range(4)]
            r128 = [sb(ph, "r128_%d" % i, [128, 256], F32) for i in range(2)]
            r64 = [sb(ph, "r64_%d" % i, [128, 128], F32) for i in range(2)]
            bm = [sb(ph, "bm%d" % i, [128, 512], F32) for i in range(2)]
            ss = sb(ph, "ss", [128, 16], F32)
            rs = sb(ph, "rs", [128, 16], F32)
            rot = {"stg": 0, "qb": 0, "r128": 0, "r64": 0, "bm": 0, "w": 0, "xt": 0}

            NSLOT = 6
            ssl = [sb(ph, "ssl%d" % i, [128, 8], F32) for i in range(NSLOT)]
            rsl = [sb(ph, "rsl%d" % i, [128, 8], F32) for i in range(NSLOT)]
            slot = [0]
            pending = []
            cur = []

            def norm_rope(src, srcb, n, w, gain, rope, ropeb, dst, dstb):
                nw_ = n * w
                k_ = slot[0] % NSLOT
                slot[0] += 1
                ss_, rs_ = ssl[k_], rsl[k_]
                sqv = sq[:, 0:nw_].rearrange("p (n w) -> p n w", n=n)
                S.op("act", lambda e: e.activation(out=sqv, in_=src, func=AF.Square), reads=[srcb], writes=[sq.b])
                S.op("dve", lambda e: e.tensor_reduce(out=ss_[:, 0:n], in_=sqv, axis=AX.X, op=ALU.add), reads=[sq.b], writes=[ss_.b])
                rstd_from_ss(ss_[:, 0:n], rs_[:, 0:n], w, ss_.b, rs_.b)

                def a2():
                    t1v = t1[:, 0:nw_].rearrange("p (n w) -> p n w", n=n)
                    S.op("dve", lambda e: e.tensor_tensor(out=t1v, in0=src, in1=rs_[:, 0:n].unsqueeze(2).to_broadcast([128, n, w]), op=ALU.mult),
                         reads=[srcb, rs_.b], writes=[t1.b])
                    gbc = gain.unsqueeze(1).to_broadcast([128, n, w])
                    if rope is None:
                        S.op("dve", lambda e: e.tensor_tensor(out=dst, in0=t1v, in1=gbc, op=ALU.mult), reads=[t1.b, GN.b], writes=[dstb])
                        return
                    S.op("dve", lambda e: e.tensor_tensor(out=t1v, in0=t1v, in1=gbc, op=ALU.mult), reads=[t1.b, GN.b], writes=[t1.b])
                    q = w // 4
                    t2v = t2[:, 0:nw_].rearrange("p (n w) -> p n w", n=n)
                    cosb = rope[:, 0:w].unsqueeze(1).to_broadcast([128, n, w])
                    S.op("dve", lambda e: e.tensor_tensor(out=t2v, in0=t1v, in1=cosb, op=ALU.mult), reads=[t1.b, ropeb], writes=[t2.b])
                    t1h = t1[:, 0:nw_].rearrange("p (n a h q) -> p n a h q", n=n, a=2, h=2)
                    t3h = t3[:, 0:nw_].rearrange("p (n a h q) -> p n a h q", n=n, a=2, h=2)
                    sinh = rope[:, w:2 * w].rearrange("p (a h q) -> p a h q", a=2, h=2)
                    for hh in range(2):
                        sv = sinh[:, :, hh, :].unsqueeze(1).to_broadcast([128, n, 2, q])
                        S.op("dve", lambda e, hh=hh, sv=sv: e.tensor_tensor(out=t3h[:, :, :, hh, :], in0=t1h[:, :, :, 1 - hh, :], in1=sv, op=ALU.mult),
                             reads=[t1.b, ropeb], writes=[t3.b])
                    t3v = t3[:, 0:nw_].rearrange("p (n w) -> p n w", n=n)
                    S.op("dve", lambda e: e.tensor_tensor(out=dst, in0=t2v, in1=t3v, op=ALU.add), reads=[t2.b, t3.b], writes=[dstb])
                cur.append(a2)

            def transposes(qbt, nblk, dst_fn, dstb):
                cur.append(lambda: transposes_now(qbt, nblk, dst_fn, dstb))

            def rotate():
                while pending:
                    pending.pop(0)()
                pending.extend(cur)
                del cur[:]

            def drain():
                rotate()
                rotate()

            def transposes_now(qbt, nblk, dst_fn, dstb):
                pb = next_pb()

                def tr(e):
                    for j in range(nblk):
                        ins = e.transpose(pb[:, j * 128:(j + 1) * 128], qbt[:, j * 128:(j + 1) * 128], identb[:, :])
                    return ins
                S.op("pe", tr, reads=[qbt.b, identb.b], writes=[pb.b])
                for j in range(nblk):
                    S.op("act", lambda e, j=j: e.activation(out=dst_fn(j), in_=pb[:, j * 128:(j + 1) * 128], func=AF.Copy), reads=[pb.b], writes=[dstb])

            for gi, gtiles in enumerate(cfg.groups):
                ng = len(gtiles)
                for ti, t in enumerate(gtiles):
                    sidx = 1 if t < NCT else 0
                    xt = XT[rot["xt"] % 2]
                    rot["xt"] += 1
                    S.dma("sp", "xt%d" % (rot["xt"] % 2), xt[:, :], xsrc(l, t), writes=[xt.b])
                    S.op("dve", lambda e, xt=xt: e.scalar_tensor_tensor(out=junk[:, :], in0=xt[:, :], scalar=1.0, in1=xt[:, :], op0=ALU.mult, op1=ALU.mult, accum_out=ss[:, 0:1]),
                         reads=[xt.b], writes=[junk.b, ss.b])
                    rstd_from_ss(ss[:, 0:1], rs[:, 0:1], D, ss.b, rs.b)
                    S.op("dve", lambda e, xt=xt: e.tensor_scalar(out=xt[:, :], in0=xt[:, :], scalar1=rs[:, 0:1], scalar2=None, op0=ALU.mult), reads=[xt.b, rs.b], writes=[xt.b])
                    for r in range(4):
                        ps = next_ps()

                        def tr(e, ps=ps, xt=xt, r=r):
                            for j in range(4):
                                k = r * 4 + j
                                ins = e.transpose(ps[:, j * 128:(j + 1) * 128], xt[:, k * 128:(k + 1) * 128], ident[:, :])
                            return ins
                        S.op("pe", tr, reads=[xt.b, ident.b], writes=[ps.b])
                        for j in range(4):
                            k = r * 4 + j
                            S.op("dve", lambda e, ps=ps, j=j, k=k, ti=ti, sidx=sidx: e.tensor_scalar(
                                out=hT[:, k, ti * 128:(ti + 1) * 128], in0=ps[:, j * 128:(j + 1) * 128],
                                scalar1=AT[:, sidx, k:k + 1], scalar2=MT[:, sidx, k:k + 1], op0=ALU.mult, op1=ALU.add),
                                reads=[ps.b, AT.b, MT.b], writes=[hT.b])

                chunks = []
                for c in range(2):
                    chunks.append(("gq", GQ + c * 512, 512, c))
                chunks.append(("gkv", GK, 512, 0))
                for c in range(4):
                    chunks.append(("mq", MQ + c * 384, 384, c))
                chunks.append(("mckv", MCKV, 512, 0))
                for c in range(2):
                    chunks.append(("uk", c * 512, 512, c))
                for c in range(2):
                    chunks.append(("uv", c * 512, 512, c))
                chunks.append(("mkr", MKR, 64, 0))
                for c in range(2):
                    chunks.append(("dq", DQ + c * 512, 512, c))
                for c in range(2):
                    chunks.append(("dk", DK + c * 512, 512, c))
                for c in range(2):
                    chunks.append(("dv", DV + c * 512, 512, c))
                for c in range(6):
                    chunks.append(("gate", GATE + c * 512, 512, c))
                for c in range(12):
                    chunks.append(("merge", MERGE + c * 512, 512, c))

                def load_w(ci):
                    kind, c0, width, c = chunks[ci]
                    w_ = wbs[ci % 2]
                    if kind == "uk":
                        src = w_uk[l, :, c0:c0 + width].rearrange("(k p) n -> p k n", p=128)
                        S.dma("pool", "pw%d" % (ci % 2), w_[:, 0:4, 0:width], src, writes=[w_.b])
                    elif kind == "uv":
                        src = w_uv[l, :, c0:c0 + width].rearrange("(k p) n -> p k n", p=128)
                        S.dma("pool", "pw%d" % (ci % 2), w_[:, 0:4, 0:width], src, writes=[w_.b])
                    else:
                        src = w_in[l, :, c0:c0 + width].rearrange("(k p) n -> p k n", p=128)
                        S.dma("pool", "pw%d" % (ci % 2), w_[:, :, 0:width], src, writes=[w_.b])

                load_w(0)
                for ci, (kind, c0, width, c) in enumerate(chunks):
                    if ci + 1 < len(chunks):
                        load_w(ci + 1)
                    w_ = wbs[ci % 2]
                    if kind == "merge":
                        bmt = bm[rot["bm"] % 2]
                        rot["bm"] += 1
                        S.dma("sp", "bm%d" % (rot["bm"] % 2), bmt[:, :], b_merge[l:l + 1, c * 512:(c + 1) * 512].partition_broadcast(128), writes=[bmt.b])
                    for h0 in range(0, ng, HTL):
                        htiles = gtiles[h0:h0 + HTL]
                        nh = len(htiles)
                        sg = stg[rot["stg"] % 3]
                        sgname = "stg%d" % (rot["stg"] % 3)
                        rot["stg"] += 1
                        sgT = sg[:, :].rearrange("p (s t) -> p s t", s=4)
                        sgV = sg[:, :].rearrange("p (s t d) -> p s t d", s=4, d=128)
                        sgM = sg[:, :].rearrange("p (t c) -> p t c", c=512)
                        for hi, t in enumerate(htiles):
                            ti = h0 + hi
                            lat = t >= NCT
                            lt = t - NCT
                            ps = next_ps()
                            lsrc = ckvT if kind in ("uk", "uv") else hT
                            nk = 4 if kind in ("uk", "uv") else KC

                            def mm(e, ps=ps, w_=w_, lsrc=lsrc, nk=nk, ti=ti, width=width):
                                for k in range(nk):
                                    ins = e.matmul(ps[:, 0:width], lhsT=lsrc[:, k, ti * 128:(ti + 1) * 128], rhs=w_[:, k, 0:width], start=(k == 0), stop=(k == nk - 1))
                                return ins
                            S.op("pe", mm, reads=[w_.b, lsrc.b], writes=[ps.b])
                            tok = slice(hi * 128, (hi + 1) * 128)
                            rp128 = rp64 = None
                            if lat and kind in ("gq", "gkv"):
                                rp128 = r128[rot["r128"] % 2]
                                rot["r128"] += 1
                                S.dma("sp", "r128_%d" % (rot["r128"] % 2), rp128[:, :], rope128[lt * 128:(lt + 1) * 128, :], writes=[rp128.b])
                            if lat and kind in ("mq", "mkr", "dq", "dk"):
                                rp64 = r64[rot["r64"] % 2]
                                rot["r64"] += 1
                                S.dma("sp", "r64_%d" % (rot["r64"] % 2), rp64[:, :], rope64[lt * 128:(lt + 1) * 128, :], writes=[rp64.b])
                            if kind == "gq":
                                q_ = qb[rot["qb"] % 4]
                                rot["qb"] += 1
                                norm_rope(ps[:, :].rearrange("p (n w) -> p n w", n=4), ps.b, 4, 128, GN[:, G_GQ:G_GQ + 128],
                                          rp128[:, :] if lat else None, rp128.b if lat else None, q_[:, :].rearrange("p (n w) -> p n w", n=4), q_.b)
                                transposes(q_, 4, lambda j, tok=tok, sgT=sgT: sgT[:, j, tok], sg.b)
                            elif kind == "gkv":
                                q_ = qb[rot["qb"] % 4]
                                rot["qb"] += 1
                                norm_rope(ps[:, 0:256].rearrange("p (n w) -> p n w", n=2), ps.b, 2, 128, GN[:, G_GK:G_GK + 128],
                                          rp128[:, :] if lat else None, rp128.b if lat else None, q_[:, 0:256].rearrange("p (n w) -> p n w", n=2), q_.b)
                                transposes(q_, 2, lambda j, tok=tok, sgT=sgT: sgT[:, j, tok], sg.b)
                                S.op("act", lambda e, ps=ps, hi=hi: e.activation(out=sgV[:, 2:4, hi, :], in_=ps[:, 256:512].rearrange("p (n w) -> p n w", n=2), func=AF.Copy),
                                     reads=[ps.b], writes=[sg.b])
                            elif kind == "mq":
                                q_ = qb[rot["qb"] % 4]
                                rot["qb"] += 1
                                pv = ps[:, 0:384].rearrange("p (n u) -> p n u", n=2)
                                norm_rope(pv[:, :, 0:128], ps.b, 2, 128, GN[:, G_MQN:G_MQN + 128], None, None,
                                          q_[:, 0:256].rearrange("p (n w) -> p n w", n=2), q_.b)
                                norm_rope(pv[:, :, 128:192], ps.b, 2, 64, GN[:, G_MQR:G_MQR + 64],
                                          rp64[:, :] if lat else None, rp64.b if lat else None, q_[:, 256:384].rearrange("p (n w) -> p n w", n=2), q_.b)
                                transposes(q_, 3, lambda j, tok=tok, sgT=sgT: sgT[:, j, tok], sg.b)
                            elif kind == "mckv":
                                q_ = qb[rot["qb"] % 4]
                                rot["qb"] += 1
                                norm_rope(ps[:, :].rearrange("p (n w) -> p n w", n=1), ps.b, 1, 512, GN[:, G_KV:G_KV + 512], None, None,
                                          q_[:, :].rearrange("p (n w) -> p n w", n=1), q_.b)
                                transposes(q_, 4, lambda j, ti=ti: ckvT[:, j, ti * 128:(ti + 1) * 128], ckvT.b)
                            elif kind == "uk":
                                q_ = qb[rot["qb"] % 4]
                                rot["qb"] += 1
                                norm_rope(ps[:, :].rearrange("p (n w) -> p n w", n=4), ps.b, 4, 128, GN[:, G_MKN:G_MKN + 128], None, None,
                                          q_[:, :].rearrange("p (n w) -> p n w", n=4), q_.b)
                                transposes(q_, 4, lambda j, tok=tok, sgT=sgT: sgT[:, j, tok], sg.b)
                            elif kind == "mkr":
                                q_ = qb[rot["qb"] % 4]
                                rot["qb"] += 1
                                norm_rope(ps[:, 0:64].rearrange("p (n w) -> p n w", n=1), ps.b, 1, 64, GN[:, G_MKR:G_MKR + 64],
                                          rp64[:, :] if lat else None, rp64.b if lat else None, q_[:, 0:64].rearrange("p (n w) -> p n w", n=1), q_.b)
                                cur.append(lambda q_=q_: S.op("dve", lambda e: e.tensor_copy(out=q_[:, 64:128], in_=q_[:, 0:64]), reads=[q_.b], writes=[q_.b]))
                                transposes(q_, 1, lambda j, tok=tok, sgT=sgT: sgT[:, j, tok], sg.b)
                            elif kind in ("dq", "dk"):
                                q_ = qb[rot["qb"] % 4]
                                rot["qb"] += 1
                                go = G_DQ if kind == "dq" else G_DK
                                norm_rope(ps[:, :].rearrange("p (n w) -> p n w", n=8), ps.b, 8, 64, GN[:, go:go + 64],
                                          rp64[:, :] if lat else None, rp64.b if lat else None, q_[:, :].rearrange("p (n w) -> p n w", n=8), q_.b)
                                transposes(q_, 4, lambda j, tok=tok, sgT=sgT: sgT[:, j, tok], sg.b)
                            elif kind in ("dv", "uv"):
                                S.op("act", lambda e, ps=ps, hi=hi: e.activation(out=sgV[:, :, hi, :], in_=ps[:, :].rearrange("p (n w) -> p n w", n=4), func=AF.Copy),
                                     reads=[ps.b], writes=[sg.b])
                            elif kind == "gate":
                                q_ = qb[rot["qb"] % 4]
                                rot["qb"] += 1
                                S.op("act", lambda e, ps=ps: e.activation(out=ee[:, :], in_=ps[:, :], func=AF.Exp, scale=-1.0), reads=[ps.b], writes=[ee.b])
                                S.op("act", lambda e: e.activation(out=ee[:, :], in_=ee[:, :], func=AF.Ln, bias=1.0), reads=[ee.b], writes=[ee.b])
                                S.op("act", lambda e: e.activation(out=ee[:, :], in_=ee[:, :], func=AF.Exp, scale=-1.0), reads=[ee.b], writes=[ee.b])
                                S.op("dve", lambda e, ps=ps, q_=q_: e.tensor_tensor(out=q_[:, :], in0=ps[:, :], in1=ee[:, :], op=ALU.mult), reads=[ps.b, ee.b], writes=[q_.b])
                                transposes(q_, 4, lambda j, tok=tok, sgT=sgT: sgT[:, j, tok], sg.b)
                            elif kind == "merge":
                                S.op("dve", lambda e, ps=ps, bmt=bmt: e.tensor_tensor(out=t1[:, :], in0=ps[:, :], in1=bmt[:, :], op=ALU.add), reads=[ps.b, bmt.b], writes=[t1.b])
                                S.op("act", lambda e: e.activation(out=ee[:, :], in_=t1[:, :], func=AF.Exp, scale=-1.0), reads=[t1.b], writes=[ee.b])
                                S.op("act", lambda e: e.activation(out=ee[:, :], in_=ee[:, :], func=AF.Ln, bias=1.0), reads=[ee.b], writes=[ee.b])
                                S.op("act", lambda e, hi=hi: e.activation(out=sgM[:, hi, :], in_=ee[:, :], func=AF.Exp, scale=-1.0), reads=[ee.b], writes=[sg.b])
                            rotate()
                        drain()
                        t0 = htiles[0]
                        tk = slice(t0 * 128, (t0 + nh) * 128)

                        def flushT(dst, h0_, ns, s0=0):
                            S.dma("sp", sgname, dst[h0_:h0_ + ns, :, tk].rearrange("h p t -> p h t"), sgT[:, s0:s0 + ns, 0:nh * 128], reads=[sg.b])

                        def flushV(dst, h0_, ns, s0=0):
                            S.dma("sp", sgname, dst[h0_:h0_ + ns, :, t0:t0 + nh, :].rearrange("h p t d -> p h t d"), sgV[:, s0:s0 + ns, 0:nh, :], reads=[sg.b])
                        if kind == "gq":
                            flushT(QT_A, 4 * c, 4)
                        elif kind == "gkv":
                            flushT(KT_A, 0, 2)
                            flushV(V_A, 0, 2, 2)
                        elif kind == "mq":
                            flushT(QT_Bn, 2 * c, 2)
                            flushT(QT_Br, c, 1, 2)
                        elif kind == "uk":
                            flushT(KT_Bn, 4 * c, 4)
                        elif kind == "uv":
                            flushV(V_B, 4 * c, 4)
                        elif kind == "mkr":
                            flushT(KT_Br, 0, 1)
                        elif kind == "dq":
                            flushT(QT_C, 4 * c, 4)
                        elif kind == "dk":
                            flushT(KT_C, 4 * c, 4)
                        elif kind == "dv":
                            flushV(V_C, 4 * c, 4)
                        elif kind == "gate":
                            flushT(Gd, 4 * c, 4)
                        elif kind == "merge":
                            S.dma("sp", sgname, Md[t0:t0 + nh, :, c * 512:(c + 1) * 512].rearrange("t p c -> p t c"), sgM[:, 0:nh, :], reads=[sg.b])
            S.barrier()

        with ExitStack() as ph:
            Kt = [sb(ph, "Kt%d" % i, [128, NTOK], BF16) for i in range(2)]
            Kz = [sb(ph, "Kz%d" % i, [128, NTOK], BF16) for i in range(2)]
            Vt = [sb(ph, "Vt%d" % i, [128, NT, 128], BF16) for i in range(2)]
            Qt = [sb(ph, "Qt%d" % i, [128, NTOK], BF16) for i in range(2)]
            Gt = [sb(ph, "Gt%d" % i, [128, NTOK], BF16) for i in range(2)]
            Kr = sb(ph, "Kr", [128, NTOK], BF16)
            Kre = [sb(ph, "Kre%d" % i, [128, NTOK], BF16) for i in range(2)]
            Qr = [sb(ph, "Qr%d" % i, [128, NTOK], BF16) for i in range(2)]
            Pt = [sb(ph, "Pt%d" % i, [128, 512], BF16) for i in range(4)]
            BGs = [sb(ph, "BGs%d" % i, [128, NTOK], BF16) for i in range(2)]
            rden = sb(ph, "rden", [128, 512], F32)
            o1 = sb(ph, "o1", [128, 512], F32)
            tt = sb(ph, "tt", [128, 512], F32)
            sqa = sb(ph, "sqa", [128, 512], F32)
            rsa = sb(ph, "rsa", [128, 512], F32)
            onesf = sb(ph, "onesf", [128, 128], F32)
            subc = sb(ph, "subc", [128, 1], F32)
            onesb = sb(ph, "onesb", [128, 128], BF16)
            S.op("pool", lambda e: e.memset(onesf[:, :], 1.0), writes=[onesf.b])
            S.op("pool", lambda e: e.memset(onesb[:, :], 1.0), writes=[onesb.b])
            for i in range(2):
                S.op("pool", lambda e, i=i: e.memset(Kz[i][:, :], 0.0), writes=[Kz[i].b])
                S.op("pool", lambda e, i=i: e.memset(Kre[i][:, :], 0.0), writes=[Kre[i].b])
            S.dma("sp", "Kr", Kr[:, :], KT_Br[0], writes=[Kr.b])
            S.op("pool", lambda e: e.tensor_copy(out=Kre[0][0:64, :], in_=Kr[0:64, :]), reads=[Kr.b], writes=[Kre[0].b])
            S.op("pool", lambda e: e.tensor_copy(out=Kre[1][64:128, :], in_=Kr[64:128, :]), reads=[Kr.b], writes=[Kre[1].b])
            S.dma("sp", "subc", subc[:, :], gains[l, G_SUB:G_SUB + 128].rearrange("(p o) -> p o", o=1), writes=[subc.b])
            S.op("dve", lambda e: e.tensor_tensor(out=subc[:, :], in0=subc[:, :], in1=LC[:, 1:2], op=ALU.mult), reads=[subc.b, LC.b], writes=[subc.b])
            prot = [0]
            uid[0] += 1
            XPS = [T(ph.enter_context(nc.psum_tensor("xps%d_%d" % (uid[0], i), [128, 512], F32)), psum=True) for i in range(2)]
            SPS = PSF[0:2] + [XPS[0]]
            OTS = PSF[2:4]
            DENS = PSF[4:6]
            srot = [0]
            orot = [0]
            drot = [0]
            for job in range(24):
                br_ = job // 8
                h = job % 8
                jb = job % 2
                K_, V_, Q_, G_ = Kt[jb], Vt[jb], Qt[jb], Gt[jb]
                if br_ == 0:
                    ksrc, vsrc, qsrc = KT_A[h // 4], V_A[h // 4], QT_A[h]
                    scale = 128 ** -0.5
                elif br_ == 1:
                    ksrc, vsrc, qsrc = KT_Bn[h], V_B[h], QT_Bn[h]
                    scale = 192 ** -0.5
                else:
                    ksrc, vsrc, qsrc = KT_C[h], V_C[h], QT_C[h]
                    scale = 64 ** -0.5
                S.dma("sp", "Kt%d" % jb, K_[:, :], ksrc, writes=[K_.b])
                S.dma("sp", "Vt%d" % jb, V_[:, :, :], vsrc, writes=[V_.b])
                S.dma("sp", "Qt%d" % jb, Q_[:, :], qsrc, writes=[Q_.b])
                S.dma("sp", "Gt%d" % jb, G_[:, :], Gd[job], writes=[G_.b])
                Qr_ = None
                if br_ == 1 and h % 2 == 0:
                    S.dma("sp", "Qr%d" % ((h // 2) % 2), Qr[(h // 2) % 2][:, :], QT_Br[h // 2], writes=[Qr[(h // 2) % 2].b])
                if br_ == 1:
                    Qr_ = Qr[(h // 2) % 2]
                if br_ == 2:
                    S.op("pool", lambda e, K_=K_: e.tensor_copy(out=Kz[0][0:64, :], in_=K_[0:64, :]), reads=[K_.b], writes=[Kz[0].b])
                    S.op("pool", lambda e, K_=K_: e.tensor_copy(out=Kz[1][64:128, :], in_=K_[64:128, :]), reads=[K_.b], writes=[Kz[1].b])
                bgs = BGs[jb]
                npass = 2 if br_ == 2 else 1
                for (qtiles, kchunks) in cfg.qblocks:
                    nq = len(qtiles)
                    qs = slice(qtiles[0] * 128, (qtiles[0] + nq) * 128)
                    qw = nq * 128
                    for pz in range(npass):
                        nkc = len(kchunks)
                        OT = OTS[orot[0] % 2]
                        orot[0] += 1
                        DENP = DENS[drot[0] % 2]
                        SSP = DENP
                        drot[0] += 1

                        def emit_S(ci):
                            kc_ = kchunks[ci]
                            sp_ = SPS[srot[0] % 3]
                            srot[0] += 1
                            ks = slice(kc_ * 128, (kc_ + 1) * 128)

                            def mm(e):
                                if br_ == 0:
                                    return e.matmul(sp_[:, 0:qw], lhsT=K_[:, ks], rhs=Q_[:, qs], start=True, stop=True)
                                if br_ == 1:
                                    e.matmul(sp_[:, 0:qw], lhsT=K_[:, ks], rhs=Q_[:, qs], start=True, stop=False)
                                    return e.matmul(sp_[:, 0:qw], lhsT=Kre[h % 2][:, ks], rhs=Qr_[:, qs], start=False, stop=True)
                                return e.matmul(sp_[:, 0:qw], lhsT=Kz[pz][:, ks], rhs=Q_[:, qs], start=True, stop=True)
                            if br_ == 0:
                                rd = [K_.b, Q_.b]
                            elif br_ == 1:
                                rd = [K_.b, Q_.b, Kre[h % 2].b, Qr_.b]
                            else:
                                rd = [Kz[pz].b, Q_.b]
                            S.op("pe", mm, reads=rd, writes=[sp_.b])
                            return sp_

                        pendq = [emit_S(0)]
                        if nkc > 1:
                            pendq.append(emit_S(1))
                        for ci in range(nkc):
                            sp_ = pendq.pop(0)
                            if ci + 2 < nkc:
                                pendq.append(emit_S(ci + 2))
                            p_ = Pt[prot[0] % 4]
                            prot[0] += 1
                            S.op("act", lambda e, sp_=sp_, p_=p_: e.activation(out=p_[:, 0:qw], in_=sp_[:, 0:qw], func=AF.Exp, scale=scale), reads=[sp_.b], writes=[p_.b])
                            kc_ = kchunks[ci]
                            wr = [OT.b, DENP.b] if (ci == 0 or ci == nkc - 1) else []

                            def pvden(e, p_=p_, kc_=kc_, ci=ci, OT=OT, DENP=DENP):
                                e.matmul(OT[:, 0:qw], lhsT=V_[:, kc_, :], rhs=p_[:, 0:qw], start=(ci == 0), stop=(ci == nkc - 1))
                                return e.matmul(DENP[:, 0:qw], lhsT=onesb[:, :], rhs=p_[:, 0:qw], start=(ci == 0), stop=(ci == nkc - 1))
                            S.op("pe", pvden, reads=[p_.b, V_.b, onesb.b], writes=wr)
                        S.op("dve", lambda e, DENP=DENP: e.reciprocal(out=rden[:, 0:qw], in_=DENP[:, 0:qw]), reads=[DENP.b], writes=[rden.b])
                        if br_ < 2:
                            S.op("dve", lambda e, OT=OT: e.tensor_tensor(out=tt[:, 0:qw], in0=OT[:, 0:qw], in1=rden[:, 0:qw], op=ALU.mult), reads=[OT.b, rden.b], writes=[tt.b])
                            S.op("dve", lambda e: e.tensor_tensor(out=bgs[:, qs], in0=tt[:, 0:qw], in1=G_[:, qs], op=ALU.mult), reads=[tt.b, G_.b], writes=[bgs.b])
                        elif pz == 0:
                            S.op("dve", lambda e, OT=OT: e.tensor_tensor(out=o1[:, 0:qw], in0=OT[:, 0:qw], in1=rden[:, 0:qw], op=ALU.mult), reads=[OT.b, rden.b], writes=[o1.b])
                        else:
                            S.op("dve", lambda e, OT=OT: e.tensor_tensor(out=tt[:, 0:qw], in0=OT[:, 0:qw], in1=rden[:, 0:qw], op=ALU.mult), reads=[OT.b, rden.b], writes=[tt.b])
                            S.op("dve", lambda e: e.scalar_tensor_tensor(out=tt[:, 0:qw], in0=tt[:, 0:qw], scalar=LAM[:, 5:6], in1=o1[:, 0:qw], op0=ALU.mult, op1=ALU.add),
                                 reads=[tt.b, LAM.b, o1.b], writes=[tt.b])
                            S.op("act", lambda e: e.activation(out=sqa[:, 0:qw], in_=tt[:, 0:qw], func=AF.Square), reads=[tt.b], writes=[sqa.b])
                            S.op("pe", lambda e, SSP=SSP: e.matmul(SSP[:, 0:qw], lhsT=onesf[:, :], rhs=sqa[:, 0:qw], start=True, stop=True), reads=[sqa.b, onesf.b], writes=[SSP.b])
                            S.op("act", lambda e, SSP=SSP: e.activation(out=rsa[:, 0:qw], in_=SSP[:, 0:qw], func=AF.Ln, scale=1.0 / 128, bias=EPS), reads=[SSP.b], writes=[rsa.b])
                            S.op("act", lambda e: e.activation(out=rsa[:, 0:qw], in_=rsa[:, 0:qw], func=AF.Exp, scale=-0.5), reads=[rsa.b], writes=[rsa.b])
                            S.op("dve", lambda e: e.tensor_tensor(out=tt[:, 0:qw], in0=tt[:, 0:qw], in1=rsa[:, 0:qw], op=ALU.mult), reads=[tt.b, rsa.b], writes=[tt.b])
                            S.op("dve", lambda e: e.scalar_tensor_tensor(out=bgs[:, qs], in0=tt[:, 0:qw], scalar=subc[:, 0:1], in1=G_[:, qs], op0=ALU.mult, op1=ALU.mult),
                                 reads=[tt.b, subc.b, G_.b], writes=[bgs.b])
                S.dma("sp", "BGs%d" % jb, BGT[job], bgs[:, :], reads=[bgs.b])
            S.barrier()

        with ExitStack() as ph:
            uid[0] += 1
            PSB[:] = [T(ph.enter_context(nc.psum_tensor("psb%d_%d" % (uid[0], i), [128, 1024], BF16)), psum=True) for i in range(2)]
            Wb = [sb(ph, "Wb%d" % i, [128, 8, D], BF16) for i in range(3)]
            bgt = [sb(ph, "bgt%d" % i, [128, 24, 128], BF16) for i in range(2)]
            mt = [sb(ph, "mt%d" % i, [128, 3 * D], BF16) for i in range(2)]
            yf = sb(ph, "yf", [128, D], F32)
            tm = sb(ph, "tm", [128, 512], F32)
            ybf = sb(ph, "ybf", [128, D], BF16)
            yTt = [sb(ph, "yTt%d" % i, [128, KC, 128], BF16) for i in range(2)]
            for i in range(3):
                for hh in range(2):
                    S.dma("pool", "Wb%d_%d" % (i, hh), Wb[i][:, hh * 4:(hh + 1) * 4, :], w_br[l, i, hh * 512:(hh + 1) * 512, :].rearrange("(h p) n -> p h n", p=128), writes=[Wb[i].b])
            for t in range(NT):
                b_ = bgt[t % 2]
                m_ = mt[t % 2]
                S.dma("sp", "bgt%d" % (t % 2), b_[:, :, :], BGT[:, :, t * 128:(t + 1) * 128].rearrange("h p t -> p h t"), writes=[b_.b])
                S.dma("sp", "mt%d" % (t % 2), m_[:, :], Md[t], writes=[m_.b])
                for n in range(4):
                    ns = slice(n * 512, (n + 1) * 512)
                    for i in range(3):
                        ps = next_ps()

                        def mm(e, ps=ps, i=i, b_=b_, ns=ns):
                            for hh in range(8):
                                ins = e.matmul(ps[:, :], lhsT=b_[:, 8 * i + hh, :], rhs=Wb[i][:, hh, ns], start=(hh == 0), stop=(hh == 7))
                            return ins
                        S.op("pe", mm, reads=[b_.b, Wb[i].b], writes=[ps.b])
                        mi = m_[:, i * D + n * 512:i * D + (n + 1) * 512]
                        if i == 0:
                            S.op("dve", lambda e, ps=ps, mi=mi, ns=ns: e.tensor_tensor(out=yf[:, ns], in0=ps[:, :], in1=mi, op=ALU.mult), reads=[ps.b, m_.b], writes=[yf.b])
                        else:
                            S.op("dve", lambda e, ps=ps, mi=mi: e.tensor_tensor(out=tm[:, :], in0=ps[:, :], in1=mi, op=ALU.mult), reads=[ps.b, m_.b], writes=[tm.b])
                            if i == 1:
                                S.op("dve", lambda e, ns=ns: e.tensor_tensor(out=yf[:, ns], in0=yf[:, ns], in1=tm[:, :], op=ALU.add), reads=[yf.b, tm.b], writes=[yf.b])
                            else:
                                S.op("dve", lambda e, ns=ns: e.tensor_tensor(out=ybf[:, ns], in0=yf[:, ns], in1=tm[:, :], op=ALU.add), reads=[yf.b, tm.b], writes=[ybf.b])
                yt_ = yTt[t % 2]
                for r in range(2):
                    pb = next_pb()

                    def tr(e, pb=pb, r=r):
                        for j in range(8):
                            k = r * 8 + j
                            ins = e.transpose(pb[:, j * 128:(j + 1) * 128], ybf[:, k * 128:(k + 1) * 128], identb[:, :])
                        return ins
                    S.op("pe", tr, reads=[ybf.b, identb.b], writes=[pb.b])
                    S.op("act", lambda e, pb=pb, r=r, yt_=yt_: e.activation(out=yt_[:, r * 8:(r + 1) * 8, :], in_=pb[:, :].rearrange("p (k t) -> p k t", k=8), func=AF.Copy),
                         reads=[pb.b], writes=[yt_.b])
                S.dma("sp", "yTt%d" % (t % 2), YT[t], yt_[:, :, :], reads=[yt_.b])
            S.barrier()

        with ExitStack() as ph:
            Wo = sb(ph, "Wo", [128, KC, D], BF16)
            gbc = sb(ph, "gbc", [128, 2, D], F32)
            yTt = [sb(ph, "yTb%d" % i, [128, KC, 128], BF16) for i in range(2)]
            xt2 = [sb(ph, "xb%d" % i, [128, D], F32) for i in range(2)]
            xo = [sb(ph, "xo%d" % i, [128, D], F32) for i in range(2)]
            tm = sb(ph, "tmb", [128, 512], F32)
            for q4 in range(4):
                S.dma("pool", "Wo%d" % q4, Wo[:, q4 * 4:(q4 + 1) * 4, :], w_out[l, q4 * 512:(q4 + 1) * 512, :].rearrange("(k p) n -> p k n", p=128), writes=[Wo.b])
            for s in range(2):
                S.dma("sp", "gbc%d" % s, gbc[:, s, :], modvec[s:s + 1, 2 * D:3 * D].partition_broadcast(128), writes=[gbc.b])
            last = (l == L - 1)
            for t in range(NT):
                if last and t < NCT and not cfg.emit_ctx:
                    continue
                sidx = 1 if t < NCT else 0
                yt_ = yTt[t % 2]
                x_ = xt2[t % 2]
                o_ = xo[t % 2]
                S.dma("sp", "yTb%d" % (t % 2), yt_[:, :, :], YT[t], writes=[yt_.b])
                S.dma("sp", "xb%d" % (t % 2), x_[:, :], xsrc(l, t), writes=[x_.b])
                for n in range(4):
                    ns = slice(n * 512, (n + 1) * 512)
                    ps = next_ps()

                    def mm(e, ps=ps, yt_=yt_, ns=ns):
                        for k in range(KC):
                            ins = e.matmul(ps[:, :], lhsT=yt_[:, k, :], rhs=Wo[:, k, ns], start=(k == 0), stop=(k == KC - 1))
                        return ins
                    S.op("pe", mm, reads=[yt_.b, Wo.b], writes=[ps.b])
                    S.op("dve", lambda e, ps=ps, ns=ns, sidx=sidx: e.tensor_tensor(out=tm[:, :], in0=ps[:, :], in1=gbc[:, sidx, ns], op=ALU.mult), reads=[ps.b, gbc.b], writes=[tm.b])
                    S.op("dve", lambda e, ns=ns, x_=x_, o_=o_: e.tensor_tensor(out=o_[:, ns], in0=x_[:, ns], in1=tm[:, :], op=ALU.add), reads=[x_.b, tm.b], writes=[o_.b])
                if last and t < NCT:
                    dst = yc_out[t * 128:(t + 1) * 128, :]
                elif last:
                    dst = y_out[(t - NCT) * 128:(t - NCT + 1) * 128, :]
                else:
                    dst = xs[t * 128:(t + 1) * 128, :]
                S.dma("sp", "xo%d" % (t % 2), dst, o_[:, :], reads=[o_.b])
            S.barrier()
    st.close()
    return nc


def rope_tables(SEQ, grid_w=64, theta=10000.0):
    t = np.arange(SEQ)
    pos_row = (t // grid_w).astype(np.float32)
    pos_col = (t % grid_w).astype(np.float32)
    out = {}
    for d in (128, 64):
        axis_dim = d // 2
        inv_freq = (np.float32(theta) ** (-np.arange(0, axis_dim, 2, dtype=np.float32) / np.float32(axis_dim))).astype(np.float32)
        ang_r = pos_row[:, None] * inv_freq
        ang_c = pos_col[:, None] * inv_freq
        ang = np.concatenate([ang_r, ang_r, ang_c, ang_c], axis=-1).astype(np.float32)
        cos = np.cos(ang).astype(np.float32)
        sin = np.sin(ang).astype(np.float32)
        q = d // 4
        sign = np.concatenate([-np.ones(q), np.ones(q), -np.ones(q), np.ones(q)]).astype(np.float32)
        out[d] = np.ascontiguousarray(np.concatenate([cos, sin * sign], axis=-1))
    return out


def make_in_maps(inp, cfg, nb, l0=0, x_cur=None, ctx_cur=None):
    L = cfg.L

    def f(a):
        a = np.asarray(a, dtype=np.float32)
        return np.ascontiguousarray(a)
    inp = dict(inp)
    for k in list(inp.keys()):
        if k not in ("x", "c", "ctx", "c_ctx"):
            inp[k] = np.asarray(inp[k])[l0:l0 + L]
    lcst = np.array([[-(0.8 - 0.6 * math.exp(-0.3 * l)), 1.0 - (0.8 - 0.6 * math.exp(-0.3 * l))] for l in range(l0, l0 + L)], dtype=np.float32)
    tabs = rope_tables(cfg.SEQ)
    gains = np.concatenate([f(inp[k]) for k in ("gqa_q_norm", "gqa_k_norm", "mla_q_nope_norm", "mla_q_rope_norm", "mla_kv_norm",
                                                "mla_k_nope_norm", "mla_k_rope_norm", "diff_q_norm", "diff_k_norm", "diff_subln")], axis=1)
    lamb = np.concatenate([f(inp[k]) for k in ("diff_lambda_q1", "diff_lambda_k1", "diff_lambda_q2", "diff_lambda_k2")], axis=1)
    shared = {
        "ident": np.eye(128, dtype=np.float32),
        "rope128": tabs[128], "rope64": tabs[64],
        "nwT": np.ascontiguousarray(f(inp["norm_w"]).reshape(L, KC, 128).transpose(0, 2, 1)),
        "w_ada": f(inp["w_ada"]), "b_ada": f(inp["b_ada"]), "w_in": f(inp["w_in"]), "b_merge": f(inp["b_merge"]),
        "gains": np.ascontiguousarray(gains), "lamb": np.ascontiguousarray(lamb), "lcst": lcst,
        "w_uk": f(inp["mla_w_uk"]), "w_uv": f(inp["mla_w_uv"]),
        "w_br": np.ascontiguousarray(np.stack([f(inp["w_br_gqa"]), f(inp["w_br_mla"]), f(inp["w_br_diff"])], axis=1)),
        "w_out": f(inp["w_out"]),
    }
    x = f(inp["x"]) if x_cur is None else x_cur
    ctx = f(inp["ctx"]) if ctx_cur is None else ctx_cur
    c = f(inp["c"])
    cc = f(inp["c_ctx"])
    maps = []
    for b in range(nb):
        cv = np.stack([c[b], cc], axis=-1)
        cvT = np.ascontiguousarray(cv.reshape(KC, 128, 2).transpose(1, 0, 2))
        m = dict(shared)
        m.update({"x": x[b], "ctx": ctx[b], "cvT": cvT})
        maps.append(m)
    return maps


def kernel(**inputs):
    cfg = Cfg(L=4)
    nc = build(cfg)
    maps = make_in_maps(inputs, cfg, 4)
    res = run_bass_kernel_spmd(nc, [maps[i] for i in range(4)], core_ids=list(range(4)))
    return np.stack([res.results[b]["y"] for b in range(4)], axis=0).astype(np.float32)
```
